# Optimizing a Trainium2 kernel written in Bass

```python
import jax, jax.numpy as jnp
from jax import lax
import numpy as np

D_MODEL = 1024
BATCH = 4
SEQ = 8192
DEPTH = 4

CTX_LEN = 256
GRID_W = 64
N_MIXERS = 2
HG_DK = 128
HG_HEADS = D_MODEL // HG_DK
HG_DV = D_MODEL // HG_HEADS
HG_CHUNK = 64
HK = HG_HEADS * HG_DK
HV = HG_HEADS * HG_DV
HG_IN = 3 * HK + 2 * HV
HG_SPLITS = [HK, HK + HV, HK + 2 * HV, 2 * HK + 2 * HV]
SGU_CHUNK = 128
SGU_WIDTH = 3 * D_MODEL
SGU_GROUPS = 8
D_FF = 128 * ((8 * D_MODEL // 3 + 127) // 128)
CONV_W = 3
N_HGRN = (DEPTH + 1) // 2
N_SGU = DEPTH // 2
ALPHA = (2.0 * DEPTH) ** 0.25
BETA = (8.0 * DEPTH) ** -0.25
LN_EPS = 1e-5
RMS_EPS = 1e-6

kernel_name = "hybrid_hgrn2_chunkgmlp_convffn_deepnorm_prefix"


def layer_norm(x, g, b):
    xf = x.astype(jnp.float32)
    mu = jnp.mean(xf, axis=-1, keepdims=True)
    var = jnp.mean(jnp.square(xf - mu), axis=-1, keepdims=True)
    return ((xf - mu) * lax.rsqrt(var + LN_EPS)).astype(x.dtype) * g + b


def modulate(h, shift, scale):
    return h * (1.0 + scale) + shift


def _heads(t, d):
    b, l, _ = t.shape
    return t.reshape(b, l, HG_HEADS, d).transpose(0, 2, 1, 3).astype(jnp.float32)


def _flip(t):
    return jnp.flip(t, axis=2)


def _forget(z_f, lb):
    lb = lb.reshape(HG_HEADS, 1, HG_DK)
    f = lb + (1.0 - lb) * jax.nn.sigmoid(_heads(z_f, HG_DK))
    return 1.0 - f, jnp.log(f)


def gla_scan(q, k, v, logf, s0):
    b, h, l, _ = q.shape
    n = l // HG_CHUNK
    to_chunks = lambda t: jnp.moveaxis(t.reshape(b, h, n, HG_CHUNK, t.shape[-1]), 2, 0)
    mask = jnp.tril(jnp.ones((HG_CHUNK, HG_CHUNK), bool))[:, :, None]

    def step(s, inp):
        qc, kc, vc, gc = inp
        G = jnp.cumsum(gc, axis=2)
        g_end = G[:, :, -1, :]
        o_inter = jnp.einsum('bhtk,bhkv->bhtv', qc * jnp.exp(G), s)
        rel = jnp.where(mask, G[:, :, :, None, :] - G[:, :, None, :, :], -jnp.inf)
        scores = jnp.einsum('bhtk,bhsk,bhtsk->bhts', qc, kc, jnp.exp(rel))
        o = o_inter + jnp.einsum('bhts,bhsv->bhtv', scores, vc)
        s = jnp.exp(g_end)[..., None] * s + jnp.einsum(
            'bhsk,bhsv->bhkv', kc * jnp.exp(g_end[:, :, None, :] - G), vc)
        return s, o

    s, o = lax.scan(step, s0, (to_chunks(q), to_chunks(k), to_chunks(v), to_chunks(logf)))
    return jnp.moveaxis(o, 0, 2).reshape(b, h, l, -1), s


def gla_final_state(k, v, logf):
    G = jnp.cumsum(logf, axis=2)
    return jnp.einsum('bhsk,bhsv->bhkv', k * jnp.exp(G[:, :, -1:, :] - G), v)


def _hg_readout(o, g, norm_w, w_out, dtype):
    o = o * lax.rsqrt(jnp.mean(o * o, axis=-1, keepdims=True) + RMS_EPS) * norm_w.astype(jnp.float32)
    b, h, l, v = o.shape
    o = o.transpose(0, 2, 1, 3).reshape(b, l, h * v).astype(dtype)
    return (o * jax.nn.silu(g)) @ w_out


def hgrn2_mixer(hx, hc, w_in, lb_fwd, lb_bwd, norm_w, w_out, ctx_out):
    dt = hx.dtype
    if ctx_out:
        qc, gc, ic, ffc, fbc = jnp.split(hc @ w_in, HG_SPLITS, axis=-1)
        qc = _heads(jax.nn.silu(qc), HG_DK)
        vc = _heads(ic, HG_DV)
        kfc, lfc = _forget(ffc, lb_fwd)
        kbc, lbc = _forget(fbc, lb_bwd)
        zero = jnp.zeros(qc.shape[:2] + (HG_DK, HG_DV), jnp.float32)
        oc_f, s_f = gla_scan(qc, kfc, vc, lfc, zero)
        oc_b, s_b = gla_scan(_flip(qc), _flip(kbc), _flip(vc), _flip(lbc), zero)
        yc = _hg_readout(oc_f + _flip(oc_b), gc, norm_w, w_out, dt)
    else:
        ic, ffc, fbc = jnp.split(hc @ w_in[:, HK + HV:], [HV, HV + HK], axis=-1)
        vc = _heads(ic, HG_DV)
        kfc, lfc = _forget(ffc, lb_fwd)
        kbc, lbc = _forget(fbc, lb_bwd)
        s_f = gla_final_state(kfc, vc, lfc)
        s_b = gla_final_state(_flip(kbc), _flip(vc), _flip(lbc))
        yc = None
    qx, gx, ix, ffx, fbx = jnp.split(hx @ w_in, HG_SPLITS, axis=-1)
    qx = _heads(jax.nn.silu(qx), HG_DK)
    vx = _heads(ix, HG_DV)
    kfx, lfx = _forget(ffx, lb_fwd)
    kbx, lbx = _forget(fbx, lb_bwd)
    ox_f, _ = gla_scan(qx, kfx, vx, lfx, s_f)
    ox_b, _ = gla_scan(_flip(qx), _flip(kbx), _flip(vx), _flip(lbx), s_b)
    yx = _hg_readout(ox_f + _flip(ox_b), gx, norm_w, w_out, dt)
    return yx, yc


def hgrn_lower_bounds(hg_lb):
    p = jax.nn.softmax(hg_lb.astype(jnp.float32), axis=1)
    return jnp.cumsum(p, axis=1) - p[:, :1]


def chunk_sgu(h, w_in, ln_g, ln_b, w_s, b_s, w_out):
    b, l, _ = h.shape
    n = l // SGU_CHUNK
    z = jax.nn.gelu(h @ w_in, approximate=False)
    u, v = jnp.split(z, 2, axis=-1)
    v = layer_norm(v, ln_g, ln_b)
    v = v.reshape(b, n, SGU_CHUNK, SGU_GROUPS, SGU_WIDTH // SGU_GROUPS)
    v = jnp.einsum('gpq,bnqgc->bnpgc', w_s, v) + b_s.T[:, :, None]
    return (u * v.reshape(b, l, SGU_WIDTH)) @ w_out


def conv_ffn(h, w_up, conv_w, conv_b, w_down, rows, row_len):
    b, l, _ = h.shape
    y = (h @ w_up).reshape(b, rows, row_len, 2 * D_FF)
    yp = jnp.pad(y, ((0, 0), (0, 0), (1, 1), (0, 0)))
    y = yp[:, :, :-2] * conv_w[0] + yp[:, :, 1:-1] * conv_w[1] + yp[:, :, 2:] * conv_w[2] + conv_b
    a, g = jnp.split(y.reshape(b, l, 2 * D_FF), 2, axis=-1)
    return (jax.nn.silu(g) * a) @ w_down


def setup_inputs(seed: int = 0) -> dict:
    key = jax.random.key(seed)
    ks = jax.random.split(key, 24)
    nrm = lambda k, shape, s: jax.random.normal(k, shape, jnp.float32) * s
    D = D_MODEL
    return {
        "x": nrm(ks[0], (BATCH, SEQ, D), 1.0),
        "c": nrm(ks[1], (BATCH, D), 1.0),
        "ctx": nrm(ks[2], (BATCH, CTX_LEN, D), 1.0),
        "c_ctx": nrm(ks[3], (D,), 1.0),
        "ada_w": nrm(ks[4], (DEPTH, D, 6 * D), D ** -0.5),
        "ada_b": nrm(ks[5], (DEPTH, 6 * D), 0.02),
        "ln_g": 1.0 + nrm(ks[6], (DEPTH, 2, D), 0.02),
        "ln_b": nrm(ks[7], (DEPTH, 2, D), 0.02),
        "hg_w_in": nrm(ks[8], (N_HGRN, D, HG_IN), D ** -0.5),
        "hg_lb": nrm(ks[9], (2, N_HGRN, HK), 0.5),
        "hg_norm_w": 1.0 + nrm(ks[10], (N_HGRN, HG_DV), 0.02),
        "hg_w_out": nrm(ks[11], (N_HGRN, HV, D), HV ** -0.5 * BETA),
        "sgu_w_in": nrm(ks[12], (N_SGU, D, 2 * SGU_WIDTH), D ** -0.5),
        "sgu_ln_g": 1.0 + nrm(ks[13], (N_SGU, SGU_WIDTH), 0.02),
        "sgu_ln_b": nrm(ks[14], (N_SGU, SGU_WIDTH), 0.02),
        "sgu_w_s": nrm(ks[15], (N_SGU, SGU_GROUPS, SGU_CHUNK, SGU_CHUNK), SGU_CHUNK ** -0.5),
        "sgu_b_s": 1.0 + nrm(ks[16], (N_SGU, SGU_GROUPS, SGU_CHUNK), 0.02),
        "sgu_w_out": nrm(ks[17], (N_SGU, SGU_WIDTH, D), SGU_WIDTH ** -0.5 * BETA),
        "ffn_w_up": nrm(ks[18], (DEPTH, D, 2 * D_FF), D ** -0.5),
        "ffn_conv_w": nrm(ks[19], (DEPTH, CONV_W, 2 * D_FF), CONV_W ** -0.5),
        "ffn_conv_b": nrm(ks[20], (DEPTH, 2 * D_FF), 0.02),
        "ffn_w_down": nrm(ks[21], (DEPTH, D_FF, D), D_FF ** -0.5 * BETA),
    }


def reference(x, c, ctx, c_ctx, ada_w, ada_b, ln_g, ln_b, hg_w_in, hg_lb, hg_norm_w, hg_w_out,
              sgu_w_in, sgu_ln_g, sgu_ln_b, sgu_w_s, sgu_b_s, sgu_w_out,
              ffn_w_up, ffn_conv_w, ffn_conv_b, ffn_w_down):
    rows = x.shape[1] // GRID_W
    ctx_len = ctx.shape[1]
    lbs = hgrn_lower_bounds(hg_lb)
    sc, scc = jax.nn.silu(c), jax.nn.silu(c_ctx)
    h, hc = x, ctx
    for layer in range(DEPTH):
        kind = layer % N_MIXERS
        idx = layer // N_MIXERS
        ctx_later = any(j % N_MIXERS == 0 for j in range(layer + 1, DEPTH))
        mx = (sc @ ada_w[layer] + ada_b[layer])[:, None, :]
        sh1, sc1, ga1, sh2, sc2, ga2 = jnp.split(mx, 6, axis=-1)
        if kind == 0 or ctx_later:
            mc = scc @ ada_w[layer] + ada_b[layer]
            csh1, csc1, cga1, csh2, csc2, cga2 = jnp.split(mc, 6, axis=-1)
        if kind == 0:
            yx, yc = hgrn2_mixer(modulate(h, sh1, sc1), modulate(hc, csh1, csc1),
                                 hg_w_in[idx], lbs[0, idx], lbs[1, idx],
                                 hg_norm_w[idx], hg_w_out[idx], ctx_later)
        else:
            yx = chunk_sgu(modulate(h, sh1, sc1), sgu_w_in[idx], sgu_ln_g[idx], sgu_ln_b[idx],
                           sgu_w_s[idx], sgu_b_s[idx], sgu_w_out[idx])
            if ctx_later:
                yc = chunk_sgu(modulate(hc, csh1, csc1), sgu_w_in[idx], sgu_ln_g[idx], sgu_ln_b[idx],
                               sgu_w_s[idx], sgu_b_s[idx], sgu_w_out[idx])
        h = layer_norm(ALPHA * h + ga1 * yx, ln_g[layer, 0], ln_b[layer, 0])
        f = conv_ffn(modulate(h, sh2, sc2), ffn_w_up[layer], ffn_conv_w[layer], ffn_conv_b[layer],
                     ffn_w_down[layer], rows, GRID_W)
        h = layer_norm(ALPHA * h + ga2 * f, ln_g[layer, 1], ln_b[layer, 1])
        if ctx_later:
            hc = layer_norm(ALPHA * hc + cga1 * yc, ln_g[layer, 0], ln_b[layer, 0])
            fc = conv_ffn(modulate(hc, csh2, csc2), ffn_w_up[layer], ffn_conv_w[layer],
                          ffn_conv_b[layer], ffn_w_down[layer], 1, ctx_len)
            hc = layer_norm(ALPHA * hc + cga2 * fc, ln_g[layer, 1], ln_b[layer, 1])
    return h
```

```python
import numpy as np
from contextlib import ExitStack
import concourse.bass as bass
import concourse.mybir as mybir
from concourse.bass_utils import run_bass_kernel_spmd

F32 = mybir.dt.float32
BF16 = mybir.dt.bfloat16
AF = mybir.ActivationFunctionType
ALU = mybir.AluOpType

ALPHA = 8.0 ** 0.25
LN_EPS = 1e-5
RMS_EPS = 1e-6
TT = 512
CT = 256
NCST = 10 * 128 + 4
class Buf:
    __slots__ = ("name", "w", "r", "dsem", "dval")

    def __init__(self, name):
        self.name = name
        self.w = {}
        self.r = {}
        self.dsem = None
        self.dval = 0


class SB:
    COMPUTE = ("pe", "act", "dve", "pool")
    NOBARRIER = ("pe", "pool")

    def __init__(self, nc):
        self.nc = nc
        self.E = {"pe": nc.tensor, "act": nc.scalar, "dve": nc.vector,
                  "pool": nc.gpsimd, "sp": nc.sync}
        self.sem = {k: nc.alloc_semaphore("sem_" + k) for k in self.E}
        self.cnt = {k: 0 for k in self.E}
        self.seen = {k: {} for k in self.E}
        self.pending = {}
        self.sem_pool = []
        self.nsem = 0
        self.ninst = 0

    @staticmethod
    def _merge(dst, deps):
        for key, d in deps.items():
            if key not in dst or dst[key][1] < d[1]:
                dst[key] = d

    def _wait(self, k, deps, raw):
        for key, (s, v, ek) in deps.items():
            if ek == k:
                if k == "pe" or not raw:
                    continue
            if self.seen[k].get(key, 0) < v:
                self.E[k].wait_ge(s, v)
                self.seen[k][key] = v

    def op(self, k, fn, R=(), W=(), inc=True):
        draw = {}
        doth = {}
        for b in R:
            self._merge(draw, b.w)
        for b in W:
            self._merge(doth, b.w)
            self._merge(doth, b.r)
        self._wait(k, draw, True)
        self._wait(k, doth, False)
        ins = fn(self.E[k])
        self.ninst += 1
        if inc:
            self.cnt[k] += 1
            ins.then_inc(self.sem[k], 1)
            v = self.cnt[k]
        else:
            v = self.cnt[k] + 1
        d = (self.sem[k], v, k)
        key = "e_" + k
        for b in R:
            b.r[key] = d
        for b in W:
            b.w[key] = d
        return ins

    def _dsem(self, b):
        if b.dsem is None:
            if self.sem_pool:
                b.dsem, b.dval = self.sem_pool.pop()
            else:
                self.nsem += 1
                b.dsem = self.nc.alloc_semaphore("dsem%d" % self.nsem)
                b.dval = 0
        return b.dsem

    def dma(self, q, out_ap, in_ap, src, dst, owner):
        deps = {}
        self._merge(deps, src.w)
        self._merge(deps, dst.w)
        self._merge(deps, dst.r)
        for key, (s, v, ek) in deps.items():
            if self.seen[q].get(key, 0) < v:
                self.E[q].wait_ge(s, v)
                self.seen[q][key] = v
        sem = self._dsem(owner)
        owner.dval += 16
        ins = self.E[q].dma_start(out=out_ap, in_=in_ap)
        ins.then_inc(sem, 16)
        self.ninst += 1
        d = (sem, owner.dval, None)
        key = "d_%d" % id(sem)
        src.r[key] = d
        dst.w[key] = d
        return ins

    def release(self, bufs):
        for b in bufs:
            self._merge(self.pending, b.w)
            self._merge(self.pending, b.r)
            if b.dsem is not None:
                self.sem_pool.append((b.dsem, b.dval))
                b.dsem = None

    def barrier(self):
        deps = dict(self.pending)
        for k in self.COMPUTE:
            deps["e_" + k] = (self.sem[k], self.cnt[k], k)
        for k in self.E:
            if k in self.NOBARRIER:
                continue
            for key, (s, v, ek) in deps.items():
                if ek == k:
                    continue
                if self.seen[k].get(key, 0) < v:
                    self.E[k].wait_ge(s, v)
                    self.seen[k][key] = v
        self.pending = {}

    def wait_all(self, k, bufs):
        deps = {}
        for b in bufs:
            self._merge(deps, b.w)
            self._merge(deps, b.r)
        for key, (s, v, ek) in deps.items():
            if self.seen[k].get(key, 0) < v:
                self.E[k].wait_ge(s, v)
                self.seen[k][key] = v


class Stage:
    UID = 0

    def __init__(self, sb, name):
        self.sb = sb
        self.name = name
        self.stk = ExitStack()
        self.bufs = []
        self.n = 0

    def __enter__(self):
        self.stk.__enter__()
        return self

    def tile(self, shape, dtype, name=None):
        Stage.UID += 1
        nm = "%s_%s%d" % (self.name, name or "t", Stage.UID)
        t = self.stk.enter_context(self.sb.nc.sbuf_tensor(nm, list(shape), dtype))
        b = Buf(nm)
        self.bufs.append(b)
        return t, b

    def __exit__(self, *a):
        self.sb.release(self.bufs)
        self.sb.barrier()
        return self.stk.__exit__(*a)


class Bank:
    def __init__(self, t, lo, b):
        self.t, self.lo, self.b = t, lo, b

    def ap(self, n=512, off=0):
        return self.t[:, self.lo + off:self.lo + off + n]


class WStream:
    def __init__(self, sb, slots):
        self.sb, self.slots = sb, slots
        self.plan = []
        self.issued = 0
        self.taken = 0

    def add(self, pieces):
        self.plan.append(pieces)

    def next(self, keep=0):
        n = self.taken
        self.taken += 1
        ns = len(self.slots)
        while self.issued < min(len(self.plan), n + ns - keep):
            t, b = self.slots[self.issued % ns]
            for (src, off, kc, ncol, db) in self.plan[self.issued]:
                dst = t[:, off:off + kc * ncol].rearrange("p (k n) -> p k n", k=kc)
                self.sb.dma("pool", dst, src, db, b, b)
            self.issued += 1
        return self.slots[n % ns]


def vec_layout(DEPTH):
    NH, NS = (DEPTH + 1) // 2, max(DEPTH // 2, 1)
    off = {}
    n = 0

    def add(key, cnt):
        nonlocal n
        off[key] = n
        n += cnt
    for l in range(DEPTH):
        add(("adab", l), 48)
        for w in range(2):
            add(("lng", l, w), 8)
            add(("lnb", l, w), 8)
        for k in range(3):
            add(("cw", l, k), 44)
        add(("cb", l), 44)
    for i in range(NH):
        add(("nw", i), 1)
    for i in range(NS):
        add(("sg", i), 24)
        add(("sb", i), 24)
    return off, n


DEBUG = False


def build(NT, DEPTH):
    NH, NS = (DEPTH + 1) // 2, max(DEPTH // 2, 1)
    VOFF, NV = vec_layout(DEPTH)
    NTT = NT * TT
    nc = bass.Bass("TRN2", target_bir_lowering=False)
    sb = SB(nc)

    def din(name, shape):
        return nc.dram_tensor(name, list(shape), F32, kind="ExternalInput").ap()

    xT = din("xT", [128, 8, NTT])
    ctxT = din("ctxT", [128, 8, CT])
    cvec = din("cvec", [128, 8, 2])
    sel = din("sel", [128, 2])
    ada_w = din("ada_w", [DEPTH, 1024, 6144])
    hg_w_in = din("hg_w_in", [NH, 1024, 5120])
    hg_w_out = din("hg_w_out", [NH, 1024, 1024])
    sgu_w_in = din("sgu_w_in", [NS, 1024, 6144])
    sgu_w_out = din("sgu_w_out", [NS, 3072, 1024])
    ffn_w_up = din("ffn_w_up", [DEPTH, 1024, 5632])
    ffn_w_down = din("ffn_w_down", [DEPTH, 2816, 1024])
    vecs = din("vecs", [128, NV])
    hg_lb = din("hg_lb", [2, 2, 1024])
    sgu_bs = din("sgu_bs", [NS, 8, 128])
    sgu_wsT = din("sgu_wsT", [NS, 128, 8, 128])
    consts = din("consts", [128, NCST])
    outT = nc.dram_tensor("outT", [128, 8, NTT], F32, kind="ExternalOutput").ap()

    WD = Buf("wdram")
    OUTB = Buf("outT")

    NTILE = NT + 1
    CTX = NT

    def tinfo(i):
        if i == CTX:
            return dict(T=CT, s=1, rl=CT, src=ctxT)
        return dict(T=TT, s=0, rl=64, src=xT[:, :, i * TT:(i + 1) * TT])

    def dscr(name, shape, dt):
        if DEBUG and not name.startswith("cc_"):
            return nc.dram_tensor(name, list(shape), dt, kind="ExternalOutput").ap(), Buf(name)
        return nc.dram_tensor(name, list(shape), dt).ap(), Buf(name)

    DBGB = Buf("dbg")

    def dump(name, t, tb, shape, dt=F32):
        if not DEBUG:
            return
        d = nc.dram_tensor("dbg_" + name, list(shape), dt, kind="ExternalOutput").ap()
        sb.dma("sp", d, t, tb, DBGB, tb)

    HSCR = [dscr("hscr%d" % i, [128, 8, TT], F32) for i in range(NT)]
    OLOC = [dscr("oloc%d" % i, [128, 8, TT], F32) for i in range(NTILE)]
    QTS = [[dscr("qts%d_%d" % (i, d), [128, 8, TT], BF16) for d in range(2)] for i in range(NTILE)]
    GATE = [dscr("gate%d" % i, [128, 8, TT], BF16) for i in range(NTILE)]
    LST = [[dscr("lst%d_%d" % (i, d), [128, 8, 128], F32) for d in range(2)] for i in range(NTILE)]
    DST = [[dscr("dst%d_%d" % (i, d), [128, 8], F32) for d in range(2)] for i in range(NTILE)]
    SIN = [[dscr("sin%d_%d" % (i, d), [128, 8, 128], BF16) for d in range(2)] for i in range(NT)]
    CCI = dscr("cc_in", [256, 1024], F32)
    CCO = dscr("cc_out", [512, 1024], F32)

    res = ExitStack()
    with res:
        def rtile(name, shape, dt):
            t = res.enter_context(nc.sbuf_tensor(name, list(shape), dt))
            return t, Buf(name)

        VEC, VECb = rtile("VEC", [128, NV], F32)
        CST, CSTb = rtile("CST", [128, NCST], F32)
        IDB, IDBb = rtile("IDB", [128, 128], BF16)
        ONESM, ONESMb = rtile("ONESM", [128, 128], F32)
        ONESR, ONESRb = rtile("ONESR", [128, 128], F32)
        MOD, MODb = rtile("MOD", [128, DEPTH, 48, 2], F32)
        DER, DERb = rtile("DER", [128, DEPTH, 2, 12, 8], F32)
        SEL, SELb = rtile("SEL", [128, 2], F32)
        slots = [rtile("wslot%d" % i, [128, 6144], BF16) for i in range(4)]
        for _t, _b in slots:
            sb._dsem(_b)
        ws = WStream(sb, slots)

        PAt = res.enter_context(nc.psum_tensor("PA", [128, 1024], F32))
        PBt = res.enter_context(nc.psum_tensor("PB", [128, 1024], F32))
        PCt = res.enter_context(nc.psum_tensor("PC", [128, 1024], F32))
        PDt = res.enter_context(nc.psum_tensor("PD", [128, 512], F32))
        PTt = res.enter_context(nc.psum_tensor("PT", [128, 1024], BF16))
        PA0, PA1 = Bank(PAt, 0, Buf("PA0")), Bank(PAt, 512, Buf("PA1"))
        PB0, PB1 = Bank(PBt, 0, Buf("PB0")), Bank(PBt, 512, Buf("PB1"))
        PC0, PC1 = Bank(PCt, 0, Buf("PC0")), Bank(PCt, 512, Buf("PC1"))
        PD = Bank(PDt, 0, Buf("PD"))
        PTb = Buf("PT")
        PTb2 = Buf("PT2")

        I_f = CST[:, 0:128]
        UBD = [CST[:, 128:256], CST[:, 256:384]]
        UTR = [CST[:, 384:512], CST[:, 512:640]]
        ONES = CST[:, 640:768]
        RMASK = CST[:, 768:772]
        AMID = [CST[:, 772:900], CST[:, 900:1028]]
        SUPM = [CST[:, 1028:1156], CST[:, 1156:1284]]

        def vcol(key, j=0, n=1):
            o = VOFF[key] + j
            return VEC[:, o:o + n]

        def der(l, s, k, c=None):
            if c is None:
                return DER[:, l, s, k, :]
            return DER[:, l, s, k, c:c + 1]

        sb.dma("sp", VEC[:], vecs, WD, VECb, VECb)
        sb.dma("sp", CST[:], consts, WD, CSTb, CSTb)
        sb.dma("sp", SEL[:], sel, WD, SELb, SELb)
        sb.op("dve", lambda e: e.tensor_copy(out=IDB[:], in_=I_f), R=[CSTb], W=[IDBb])
        sb.op("dve", lambda e: e.memset(ONESM[:], 1.0 / 1024.0), W=[ONESMb])
        sb.op("dve", lambda e: e.memset(ONESR[:], 1.0 / 128.0), W=[ONESRb])

        with Stage(sb, "mod") as st:
            CV, CVb = st.tile([128, 8, 2], F32)
            SCV, SCVb = st.tile([128, 8, 2], F32)
            AW = [st.tile([128, 8, 512], F32) for _ in range(2)]
            sb.dma("sp", CV[:], cvec, WD, CVb, CVb)
            sb.op("act", lambda e: e.activation(out=SCV[:], in_=CV[:], func=AF.Silu), R=[CVb], W=[SCVb])
            gi = 0
            for l in range(DEPTH):
                for g in range(12):
                    A, Ab = AW[gi % 2]
                    gi += 1
                    sb.dma("sp", A[:], ada_w[l, :, g * 512:(g + 1) * 512].rearrange("(k p) n -> p k n", p=128), WD, Ab, Ab)
                    for f in range(4):
                        fc = g * 4 + f
                        for kc in range(8):
                            sb.op("pe", lambda e, A=A, f=f, kc=kc, fc=fc: e.matmul(
                                PD.ap(2, fc * 2), lhsT=A[:, kc, f * 128:(f + 1) * 128], rhs=SCV[:, kc, :],
                                start=(kc == 0), stop=(kc == 7)), R=[Ab, SCVb], W=[PD.b], inc=(kc == 7))
                sb.op("dve", lambda e, l=l: e.tensor_tensor(
                    out=MOD[:, l, :, :], in0=PD.ap(96).rearrange("p (c s) -> p c s", s=2),
                    in1=vcol(("adab", l), 0, 48).unsqueeze(2).to_broadcast([128, 48, 2]), op=ALU.add),
                    R=[PD.b, VECb], W=[MODb])
            for l in range(DEPTH):
                last = (l == DEPTH - 1)
                a2 = 1.0 if last else ALPHA
                for s in range(2):
                    def m(j):
                        return MOD[:, l, j * 8:(j + 1) * 8, s]

                    def mn(j):
                        return MOD[:, l + 1, j * 8:(j + 1) * 8, s]
                    W_ = [DERb]
                    R_ = [MODb, VECb, DERb]
                    sb.op("dve", lambda e: e.tensor_scalar_add(out=der(l, s, 0), in0=m(1), scalar1=1.0), R=R_, W=W_)
                    sb.op("dve", lambda e: e.tensor_copy(out=der(l, s, 1), in_=m(0)), R=R_, W=W_)
                    sb.op("dve", lambda e: e.tensor_copy(out=der(l, s, 2), in_=m(2)), R=R_, W=W_)
                    sb.op("dve", lambda e: e.tensor_copy(out=der(l, s, 3), in_=m(5)), R=R_, W=W_)
                    g1, b1 = vcol(("lng", l, 0), 0, 8), vcol(("lnb", l, 0), 0, 8)
                    g2, b2 = vcol(("lng", l, 1), 0, 8), vcol(("lnb", l, 1), 0, 8)
                    sb.op("dve", lambda e: e.tensor_scalar_mul(out=der(l, s, 4), in0=g1, scalar1=ALPHA), R=R_, W=W_)
                    sb.op("dve", lambda e: e.tensor_scalar_mul(out=der(l, s, 5), in0=b1, scalar1=ALPHA), R=R_, W=W_)
                    sb.op("dve", lambda e: e.scalar_tensor_tensor(out=der(l, s, 6), in0=m(4), scalar=1.0, in1=g1, op0=ALU.add, op1=ALU.mult), R=R_, W=W_)
                    sb.op("dve", lambda e: e.scalar_tensor_tensor(out=der(l, s, 7), in0=m(4), scalar=1.0, in1=b1, op0=ALU.add, op1=ALU.mult), R=R_, W=W_)
                    sb.op("dve", lambda e: e.tensor_tensor(out=der(l, s, 7), in0=der(l, s, 7), in1=m(3), op=ALU.add), R=R_, W=W_)
                    sb.op("dve", lambda e: e.tensor_scalar_mul(out=der(l, s, 8), in0=g2, scalar1=a2), R=R_, W=W_)
                    sb.op("dve", lambda e: e.tensor_scalar_mul(out=der(l, s, 9), in0=b2, scalar1=a2), R=R_, W=W_)
                    if not last:
                        sb.op("dve", lambda e: e.scalar_tensor_tensor(out=der(l, s, 10), in0=mn(1), scalar=1.0, in1=g2, op0=ALU.add, op1=ALU.mult), R=R_, W=W_)
                        sb.op("dve", lambda e: e.scalar_tensor_tensor(out=der(l, s, 11), in0=mn(1), scalar=1.0, in1=b2, op0=ALU.add, op1=ALU.mult), R=R_, W=W_)
                        sb.op("dve", lambda e: e.tensor_tensor(out=der(l, s, 11), in0=der(l, s, 11), in1=mn(0), op=ALU.add), R=R_, W=W_)

        dump('MOD', MOD[:], MODb, [128, DEPTH, 48, 2])
        dump('DER', DER[:], DERb, [128, DEPTH, 2, 12, 8])
        def wsrc(w, c0, n):
            return w[:, c0:c0 + n].rearrange("(k p) n -> p k n", p=128)

        def plan_hg1(idx):
            for g in range(10):
                ws.add([(wsrc(hg_w_in[idx], g * 512, 512), 0, 8, 512, WD)])

        def plan_hg2(idx):
            for g in range(2):
                ws.add([(wsrc(hg_w_out[idx], g * 512, 512), 0, 8, 512, WD)])

        def plan_ffn(l):
            for g in range(11):
                ws.add([(wsrc(ffn_w_up[l], g * 256, 256), 0, 8, 256, WD),
                        (wsrc(ffn_w_up[l], 2816 + g * 256, 256), 2048, 8, 256, WD)])
            for g in range(4):
                ws.add([(wsrc(ffn_w_down[l], g * 256, 256), 0, 22, 256, WD)])

        def plan_sgu(idx):
            for g in range(6):
                ws.add([(wsrc(sgu_w_in[idx], 3072 + g * 512, 512), 0, 8, 512, WD)])
            for g in range(6):
                ws.add([(wsrc(sgu_w_in[idx], g * 512, 512), 0, 8, 512, WD)])
            for g in range(4):
                ws.add([(wsrc(sgu_w_out[idx], g * 256, 256), 0, 24, 256, WD)])

        def wview(slot, off, kc, ncol):
            return slot[0][:, off:off + kc * ncol].rearrange("p (k n) -> p k n", k=kc)

        class LN:
            def __init__(self, st, T, HA, HAb, HM, HMb, l, s, which, dve_sum=False):
                self.T, self.HA, self.HAb, self.HM, self.HMb = T, HA, HAb, HM, HMb
                self.l, self.s = l, s
                self.kga = 2 if which == 0 else 3
                self.kb = 4 if which == 0 else 8
                self.ZSQ = [st.tile([128, T], F32) for _ in range(2)]
                self.MS = st.tile([128, T], F32)
                self.RS = st.tile([128, T], F32)
                self.XC = [st.tile([128, T], F32) for _ in range(2)]
                self.has_hm = not (which == 1 and l == DEPTH - 1)
                self.mean, self.ex2 = PC0, PC1
                self.dve_sum = dve_sum
                if dve_sum:
                    self.ZS = st.tile([128, T], F32)
                    self.ZQ = st.tile([128, T], F32)

            def accum(self, c, ybank):
                T = self.T
                HA, HAb = self.HA, self.HAb
                sb.op("dve", lambda e: e.scalar_tensor_tensor(
                    out=HA[:, c, :], in0=ybank.ap(T), scalar=der(self.l, self.s, self.kga, c), in1=HA[:, c, :],
                    op0=ALU.mult, op1=ALU.add), R=[ybank.b, DERb, HAb], W=[HAb])
                if self.dve_sum:
                    ZS, ZSb = self.ZS
                    ZQ, ZQb = self.ZQ
                    if c == 0:
                        sb.op("act", lambda e: e.activation(out=ZQ[:], in_=HA[:, c, :], func=AF.Square), R=[HAb], W=[ZQb])
                    else:
                        Z, Zb = self.ZSQ[c % 2]
                        sb.op("act", lambda e: e.activation(out=Z[:], in_=HA[:, c, :], func=AF.Square), R=[HAb], W=[Zb])
                        sb.op("dve", lambda e: e.tensor_tensor(out=ZQ[:], in0=ZQ[:], in1=Z[:], op=ALU.add), R=[ZQb, Zb], W=[ZQb])
                        if c == 1:
                            sb.op("dve", lambda e: e.tensor_tensor(out=ZS[:], in0=HA[:, 0, :], in1=HA[:, 1, :], op=ALU.add), R=[HAb], W=[ZSb])
                        else:
                            sb.op("dve", lambda e: e.tensor_tensor(out=ZS[:], in0=ZS[:], in1=HA[:, c, :], op=ALU.add), R=[ZSb, HAb], W=[ZSb])
                    if c == 7:
                        sb.op("pe", lambda e: e.matmul(self.mean.ap(T), lhsT=ONESM[:], rhs=ZS[:], start=True, stop=True), R=[ONESMb, ZSb], W=[self.mean.b])
                        sb.op("pe", lambda e: e.matmul(self.ex2.ap(T), lhsT=ONESM[:], rhs=ZQ[:], start=True, stop=True), R=[ONESMb, ZQb], W=[self.ex2.b])
                    return
                Z, Zb = self.ZSQ[c % 2]
                sb.op("act", lambda e: e.activation(out=Z[:], in_=HA[:, c, :], func=AF.Square), R=[HAb], W=[Zb])
                sb.op("pe", lambda e: e.matmul(self.mean.ap(T), lhsT=ONESM[:], rhs=HA[:, c, :], start=(c == 0), stop=(c == 7)),
                      R=[ONESMb, HAb], W=[self.mean.b], inc=(c == 7))
                sb.op("pe", lambda e: e.matmul(self.ex2.ap(T), lhsT=ONESM[:], rhs=Z[:], start=(c == 0), stop=(c == 7)),
                      R=[ONESMb, Zb], W=[self.ex2.b], inc=True)

            def finish(self):
                T = self.T
                HA, HAb, HM, HMb = self.HA, self.HAb, self.HM, self.HMb
                MS, MSb = self.MS
                RS, RSb = self.RS
                sb.op("act", lambda e: e.activation(out=MS[:], in_=self.mean.ap(T), func=AF.Square), R=[self.mean.b], W=[MSb])
                sb.op("dve", lambda e: e.tensor_tensor(out=RS[:], in0=self.ex2.ap(T), in1=MS[:], op=ALU.subtract), R=[self.ex2.b, MSb], W=[RSb])
                sb.op("act", lambda e: e.activation(out=RS[:], in_=RS[:], func=AF.Sqrt, bias=LN_EPS), R=[RSb], W=[RSb])
                sb.op("dve", lambda e: e.reciprocal(out=RS[:], in_=RS[:]), R=[RSb], W=[RSb])
                for c in range(8):
                    X, Xb = self.XC[c % 2]
                    sb.op("dve", lambda e: e.tensor_tensor(out=X[:], in0=HA[:, c, :], in1=self.mean.ap(T), op=ALU.subtract), R=[HAb, self.mean.b], W=[Xb])
                    sb.op("dve", lambda e: e.tensor_tensor(out=X[:], in0=X[:], in1=RS[:], op=ALU.mult), R=[Xb, RSb], W=[Xb])
                    sb.op("act", lambda e: e.activation(out=HA[:, c, :], in_=X[:], func=AF.Identity,
                                                        scale=der(self.l, self.s, self.kb, c), bias=der(self.l, self.s, self.kb + 1, c)),
                          R=[Xb, DERb], W=[HAb])
                    if self.has_hm:
                        sb.op("act", lambda e: e.activation(out=HM[:, c, :], in_=X[:], func=AF.Identity,
                                                            scale=der(self.l, self.s, self.kb + 2, c), bias=der(self.l, self.s, self.kb + 3, c)),
                              R=[Xb, DERb], W=[HMb[c]])

        YB = [PA0, PA1, PB0, PB1]

        def run_ffn(l, ti, HA, HAb, HM, HMb):
            T, s, rl = ti["T"], ti["s"], ti["rl"]
            with Stage(sb, "ffn") as st:
                U, Ub = st.tile([128, 22, T], BF16)
                AT = [st.tile([128, T], F32) for _ in range(2)]
                GT = [st.tile([128, T], F32) for _ in range(2)]
                ln = LN(st, T, HA, HAb, HM, HMb, l, s, 1, dve_sum=True)
                ybi = 0

                def conv(bank, X, Xb, j):
                    w0, w1, w2, cb = vcol(("cw", l, 0), j), vcol(("cw", l, 1), j), vcol(("cw", l, 2), j), vcol(("cb", l), j)
                    sb.op("act", lambda e: e.activation(out=X[:], in_=bank.ap(T), func=AF.Identity, scale=w1, bias=cb), R=[bank.b, VECb], W=[Xb])
                    pv = bank.ap(T).rearrange("p (r l) -> p r l", l=rl)
                    xv = X[:].rearrange("p (r l) -> p r l", l=rl)
                    sb.op("dve", lambda e: e.scalar_tensor_tensor(out=xv[:, :, 1:rl], in0=pv[:, :, 0:rl - 1], scalar=w0, in1=xv[:, :, 1:rl],
                                                                   op0=ALU.mult, op1=ALU.add), R=[bank.b, VECb, Xb], W=[Xb])
                    sb.op("dve", lambda e: e.scalar_tensor_tensor(out=xv[:, :, 0:rl - 1], in0=pv[:, :, 1:rl], scalar=w2, in1=xv[:, :, 0:rl - 1],
                                                                   op0=ALU.mult, op1=ALU.add), R=[bank.b, VECb, Xb], W=[Xb])

                for g in range(11):
                    slot = ws.next()
                    wa = wview(slot, 0, 8, 256)
                    wg = wview(slot, 2048, 8, 256)
                    for jj in range(2):
                        j = 2 * g + jj
                        ba, bg = YB[ybi % 4], YB[(ybi + 1) % 4]
                        ybi += 2
                        for kc in range(8):
                            sb.op("pe", lambda e: e.matmul(ba.ap(T), lhsT=wa[:, kc, jj * 128:(jj + 1) * 128], rhs=HM[:, kc, :], start=(kc == 0), stop=(kc == 7)),
                                  R=[slot[1], HMb[kc]], W=[ba.b], inc=(kc == 7))
                        for kc in range(8):
                            sb.op("pe", lambda e: e.matmul(bg.ap(T), lhsT=wg[:, kc, jj * 128:(jj + 1) * 128], rhs=HM[:, kc, :], start=(kc == 0), stop=(kc == 7)),
                                  R=[slot[1], HMb[kc]], W=[bg.b], inc=(kc == 7))
                        A, Ab = AT[j % 2]
                        G, Gb = GT[j % 2]
                        conv(ba, A, Ab, j)
                        conv(bg, G, Gb, 22 + j)
                        sb.op("act", lambda e: e.activation(out=G[:], in_=G[:], func=AF.Silu), R=[Gb], W=[Gb])
                        sb.op("dve", lambda e: e.tensor_tensor(out=U[:, j, :], in0=G[:], in1=A[:], op=ALU.mult), R=[Gb, Ab], W=[Ub])
                for g in range(4):
                    slot = ws.next()
                    wd = wview(slot, 0, 22, 256)
                    for oo in range(2):
                        oc = 2 * g + oo
                        yb = YB[ybi % 4]
                        ybi += 1
                        for j in range(22):
                            sb.op("pe", lambda e: e.matmul(yb.ap(T), lhsT=wd[:, j, oo * 128:(oo + 1) * 128], rhs=U[:, j, :], start=(j == 0), stop=(j == 21)),
                                  R=[slot[1], Ub], W=[yb.b], inc=(j == 21))
                        ln.accum(oc, yb)
                ln.finish()

        def run_sgu(l, ti, HA, HAb, HM, HMb):
            idx = l // 2
            T, s = ti["T"], ti["s"]
            NG = T // 128
            with Stage(sb, "sgu") as st:
                VT, VTb = st.tile([128, NG, 3072], BF16)
                UF, UFb = st.tile([128, 24, T], BF16)
                WSF, WSFb = st.tile([128, 8, 128], F32)
                WSB, WSBb = st.tile([128, 8, 128], BF16)
                BSB, BSBb = st.tile([128, 8, 128], F32)
                BT, BTb = st.tile([128, 24, 128], F32)
                GEL = [st.tile([128, 512], F32) for _ in range(2)]
                STATS, STATSb = st.tile([128, NG, 6, 6], F32)
                MV, MVb = st.tile([128, NG, 2], F32)
                TMP = [st.tile([128, T], F32) for _ in range(2)]
                ln = LN(st, T, HA, HAb, HM, HMb, l, s, 0, dve_sum=True)
                sb.dma("sp", WSF[:], sgu_wsT[idx], WD, WSFb, WSFb)
                sb.op("dve", lambda e: e.tensor_copy(out=WSB[:], in_=WSF[:]), R=[WSFb], W=[WSBb])
                sb.dma("sp", BSB[:].rearrange("p g q -> p (g q)"), sgu_bs[idx].rearrange("g q -> (g q)").partition_broadcast(128), WD, BSBb, BSBb)
                for g in range(8):
                    bank = PD
                    sb.op("pe", lambda e: e.matmul(bank.ap(128), lhsT=ONES, rhs=WSF[:, g, :], start=True, stop=True), R=[CSTb, WSFb], W=[bank.b])
                    for cc in range(3):
                        c = 3 * g + cc
                        sb.op("dve", lambda e: e.scalar_tensor_tensor(out=BT[:, c, :], in0=bank.ap(128), scalar=vcol(("sb", idx), c), in1=BSB[:, g, :],
                                                                       op0=ALU.mult, op1=ALU.add), R=[bank.b, VECb, BSBb], W=[BTb])
                vb = [PA0, PA1, PB0, PB1]
                vi = 0
                for g in range(6):
                    slot = ws.next()
                    w = wview(slot, 0, 8, 512)
                    for tg in range(NG):
                        bank = vb[vi % 4]
                        vi += 1
                        for kc in range(8):
                            sb.op("pe", lambda e: e.matmul(bank.ap(512), lhsT=HM[:, kc, tg * 128:(tg + 1) * 128], rhs=w[:, kc, :], start=(kc == 0), stop=(kc == 7)),
                                  R=[slot[1], HMb[kc]], W=[bank.b], inc=(kc == 7))
                        GE, GEb = GEL[vi % 2]
                        sb.op("act", lambda e: e.activation(out=GE[:], in_=bank.ap(512), func=AF.Gelu), R=[bank.b], W=[GEb])
                        sb.op("dve", lambda e: e.bn_stats(out=STATS[:, tg, g, :], in_=GE[:]), R=[GEb], W=[STATSb])
                        sb.op("dve", lambda e: e.tensor_copy(out=VT[:, tg, g * 512:(g + 1) * 512], in_=GE[:]), R=[GEb], W=[VTb])
                for tg in range(NG):
                    sb.op("dve", lambda e: e.bn_aggr(out=MV[:, tg, :], in_=STATS[:, tg, :, :].rearrange("p a b -> p (a b)")), R=[STATSb], W=[MVb])
                sb.op("act", lambda e: e.activation(out=MV[:, :, 1], in_=MV[:, :, 1], func=AF.Sqrt, bias=LN_EPS), R=[MVb], W=[MVb])
                sb.op("dve", lambda e: e.reciprocal(out=MV[:, :, 1], in_=MV[:, :, 1]), R=[MVb], W=[MVb])
                for tg in range(NG):
                    sb.op("dve", lambda e: e.tensor_scalar(out=VT[:, tg, :], in0=VT[:, tg, :], scalar1=MV[:, tg, 0:1], scalar2=MV[:, tg, 1:2],
                                                           op0=ALU.subtract, op1=ALU.mult), R=[VTb, MVb], W=[VTb])
                ub = [PA0, PA1]
                mb = [PB0, PB1]
                for g in range(6):
                    slot = ws.next()
                    w = wview(slot, 0, 8, 512)
                    for f in range(4):
                        c = 4 * g + f
                        bank = ub[c % 2]
                        for kc in range(8):
                            sb.op("pe", lambda e: e.matmul(bank.ap(T), lhsT=w[:, kc, f * 128:(f + 1) * 128], rhs=HM[:, kc, :], start=(kc == 0), stop=(kc == 7)),
                                  R=[slot[1], HMb[kc]], W=[bank.b], inc=(kc == 7))
                        sb.op("act", lambda e: e.activation(out=UF[:, c, :], in_=bank.ap(T), func=AF.Gelu), R=[bank.b], W=[UFb])
                        mbank = mb[c % 2]
                        grp = c // 3
                        for tg in range(NG):
                            sb.op("pe", lambda e: e.matmul(mbank.ap(128, tg * 128), lhsT=VT[:, tg, c * 128:(c + 1) * 128], rhs=WSB[:, grp, :], start=True, stop=True),
                                  R=[VTb, WSBb], W=[mbank.b], inc=(tg == NG - 1))
                        TM, TMb = TMP[c % 2]
                        sb.op("dve", lambda e: e.scalar_tensor_tensor(
                            out=TM[:].rearrange("p (g q) -> p g q", q=128), in0=mbank.ap(T).rearrange("p (g q) -> p g q", q=128),
                            scalar=vcol(("sg", idx), c), in1=BT[:, c:c + 1, :].to_broadcast([128, NG, 128]),
                            op0=ALU.mult, op1=ALU.add), R=[mbank.b, VECb, BTb], W=[TMb])
                        sb.op("dve", lambda e: e.tensor_tensor(out=UF[:, c, :], in0=UF[:, c, :], in1=TM[:], op=ALU.mult), R=[UFb, TMb], W=[UFb])
                ybi = 0
                for g in range(4):
                    slot = ws.next()
                    wo = wview(slot, 0, 24, 256)
                    for oo in range(2):
                        oc = 2 * g + oo
                        yb = YB[ybi % 4]
                        ybi += 1
                        for j in range(24):
                            sb.op("pe", lambda e: e.matmul(yb.ap(T), lhsT=wo[:, j, oo * 128:(oo + 1) * 128], rhs=UF[:, j, :], start=(j == 0), stop=(j == 23)),
                                  R=[slot[1], UFb], W=[yb.b], inc=(j == 23))
                        ln.accum(oc, yb)
                ln.finish()

        def run_hg1(l, i, ti, HM, HMb):
            idx = l // 2
            T = ti["T"]
            NG = T // 128
            with Stage(sb, "hg1") as so:
                Q, Qb = so.tile([128, 8, T], F32)
                VT, VTb = so.tile([128, NG, 1024], BF16)
                OL, OLb = so.tile([128, 8, T], F32)
                with Stage(sb, "hg1a") as st:
                    GTt, GTb = st.tile([128, 8, T], BF16)
                    banks = [PA0, PA1, PB0, PB1]
                    bi = 0
                    for g in range(6):
                        slot = ws.next()
                        w = wview(slot, 0, 8, 512)
                        if g < 4:
                            for f in range(4):
                                h = (g % 2) * 4 + f
                                bank = banks[bi % 4]
                                bi += 1
                                for kc in range(8):
                                    sb.op("pe", lambda e: e.matmul(bank.ap(T), lhsT=w[:, kc, f * 128:(f + 1) * 128], rhs=HM[:, kc, :], start=(kc == 0), stop=(kc == 7)),
                                          R=[slot[1], HMb[kc]], W=[bank.b], inc=(kc == 7))
                                if g < 2:
                                    sb.op("act", lambda e: e.activation(out=Q[:, h, :], in_=bank.ap(T), func=AF.Silu), R=[bank.b], W=[Qb])
                                else:
                                    sb.op("act", lambda e: e.activation(out=GTt[:, h, :], in_=bank.ap(T), func=AF.Silu), R=[bank.b], W=[GTb])
                        else:
                            for tg in range(NG):
                                bank = banks[bi % 4]
                                bi += 1
                                for kc in range(8):
                                    sb.op("pe", lambda e: e.matmul(bank.ap(512), lhsT=HM[:, kc, tg * 128:(tg + 1) * 128], rhs=w[:, kc, :], start=(kc == 0), stop=(kc == 7)),
                                          R=[slot[1], HMb[kc]], W=[bank.b], inc=(kc == 7))
                                sb.op("act", lambda e: e.activation(out=VT[:, tg, (g - 4) * 512:(g - 3) * 512], in_=bank.ap(512), func=AF.Identity), R=[bank.b], W=[VTb])
                    sb.dma("sp", GATE[i][0][:, :, 0:T], GTt[:], GTb, GATE[i][1], GTb)
                OLB = [Buf("olA"), Buf("olB")]
                so.bufs.extend(OLB)
                for d in range(2):
                    with Stage(sb, "hg1b") as st:
                        def T2(shape, dt):
                            return [st.tile(shape, dt) for _ in range(2)]
                        QTI = T2([128, 4, T], BF16)
                        LOGF = T2([128, 512], F32)
                        KK = T2([128, 512], F32)
                        CAR = T2([128, 512], F32)
                        VM = T2([128, 4, 512], BF16)
                        ENG = T2([128, 512], F32)
                        KT = T2([128, 512], BF16)
                        KTT = T2([128, 4, 128], BF16)
                        KH = T2([128, 512], BF16)
                        QTS_ = T2([128, 4, 128], BF16)
                        EG = T2([128, 4, 128], F32)
                        QTL = T2([128, 4, 128], BF16)
                        EGT = T2([128, 4, 128], F32)
                        SCM = T2([128, 4, 128], BF16)
                        S = T2([128, 4, 128], F32)
                        SBF = T2([128, 4, 128], BF16)
                        TMP = T2([128, 4, 128], F32)
                        FF = T2([128, 512], F32)
                        DT = T2([128, 4], F32)
                        MSK, MSKb = st.tile([128, 4, 128], F32)
                        OM = T2([128, 512], F32) if idx > 0 else None
                        sb.op("dve", lambda e: e.tensor_copy(out=MSK[:], in_=UBD[d].unsqueeze(1).to_broadcast([128, 4, 128])), R=[CSTb], W=[MSKb])
                        wslots = [ws.next(), ws.next(keep=1)]
                        order = list(range(NG)) if d == 0 else list(range(NG - 1, -1, -1))
                        jorder = [0, 1, 2, 3] if d == 0 else [3, 2, 1, 0]
                        PTB = [PTb, PTb2]

                        def chain(hh):
                            hs = hh * 4
                            cs = slice(hh * 512, (hh + 1) * 512)
                            PAh, PBh, PCh = (PA0, PA1)[hh], (PB0, PB1)[hh], (PC0, PC1)[hh]
                            PTh, PThb = PTt[:, hh * 512:(hh + 1) * 512], PTB[hh]
                            slot = wslots[hh]
                            w = wview(slot, 0, 8, 512)
                            qti, qtib = QTI[hh]
                            logf, logfb = LOGF[hh]
                            kk, kkb = KK[hh]
                            car, carb = CAR[hh]
                            vm, vmb = VM[hh]
                            eng, engb = ENG[hh]
                            kt, ktb = KT[hh]
                            ktt, kttb = KTT[hh]
                            kh, khb = KH[hh]
                            qts, qtsb = QTS_[hh]
                            eg, egb = EG[hh]
                            qtl, qtlb = QTL[hh]
                            egt, egtb = EGT[hh]
                            scm, scmb = SCM[hh]
                            s_, s_b = S[hh]
                            sbf, sbfb = SBF[hh]
                            tmp, tmpb = TMP[hh]
                            ff, ffb = FF[hh]
                            dt_, dtb = DT[hh]
                            olb = OLB[hh]
                            f2 = lambda a: a.rearrange("p h s -> p (h s)")
                            v3 = lambda a: a.rearrange("p (h s) -> p h s", h=4)
                            sb.op("dve", lambda e: e.memset(scm[:], 0.0), W=[scmb])
                            if idx > 0:
                                om, omb = OM[hh]
                                sb.dma("sp", om[:], hg_lb[d, 0, cs].partition_broadcast(128), WD, omb, omb)
                                sb.dma("sp", eng[:], hg_lb[d, 1, cs].partition_broadcast(128), WD, engb, engb)
                                yield
                                sb.op("dve", lambda e: e.tensor_tensor(out=om[:], in0=om[:], in1=eng[:], op=ALU.subtract), R=[omb, engb], W=[omb])
                                yield
                                sb.op("act", lambda e: e.activation(out=om[:], in_=om[:], func=AF.Sigmoid), R=[omb], W=[omb])
                                yield
                            first_step = True
                            for gi_, tg in enumerate(order):
                                tsl = slice(tg * 128, (tg + 1) * 128)
                                for kc in range(8):
                                    sb.op("pe", lambda e: e.matmul(PAh.ap(512), lhsT=HM[:, kc, tsl], rhs=w[:, kc, :], start=(kc == 0), stop=(kc == 7)),
                                          R=[slot[1], HMb[kc]], W=[PAh.b], inc=(kc == 7))
                                yield
                                if idx == 0:
                                    sb.op("act", lambda e: e.activation(out=ff[:], in_=PAh.ap(512), func=AF.Sigmoid), R=[PAh.b], W=[ffb])
                                    sb.op("act", lambda e: e.activation(out=kk[:], in_=PAh.ap(512), func=AF.Sigmoid, scale=-1.0), R=[PAh.b], W=[kkb])
                                    yield
                                else:
                                    sb.op("act", lambda e: e.activation(out=ff[:], in_=PAh.ap(512), func=AF.Sigmoid, scale=-1.0), R=[PAh.b], W=[ffb])
                                    yield
                                    sb.op("dve", lambda e: e.tensor_tensor(out=kk[:], in0=ff[:], in1=om[:], op=ALU.mult), R=[ffb, omb], W=[kkb])
                                    yield
                                    sb.op("dve", lambda e: e.tensor_scalar(out=ff[:], in0=kk[:], scalar1=-1.0, scalar2=1.0, op0=ALU.mult, op1=ALU.add), R=[kkb], W=[ffb])
                                    yield
                                sb.op("act", lambda e: e.activation(out=logf[:], in_=ff[:], func=AF.Ln), R=[ffb], W=[logfb])
                                yield
                                for j in range(4):
                                    sb.op("act", lambda e: e.activation(out=vm[:, j, :], in_=VT[:, tg, cs], func=AF.Identity, scale=RMASK[:, j:j + 1]), R=[VTb, CSTb], W=[vmb])
                                yield
                                sb.op("pe", lambda e: e.matmul(PAh.ap(512), lhsT=AMID[d], rhs=logf[:], start=True, stop=True), R=[CSTb, logfb], W=[PAh.b])
                                yield
                                sb.op("act", lambda e: e.activation(out=eng[:], in_=PAh.ap(512), func=AF.Exp, scale=-1.0), R=[PAh.b], W=[engb])
                                yield
                                sb.op("dve", lambda e: e.tensor_tensor(out=kt[:], in0=kk[:], in1=eng[:], op=ALU.mult), R=[kkb, engb], W=[ktb])
                                yield
                                sb.op("pe", lambda e: e.matmul(PAh.ap(512), lhsT=SUPM[d], rhs=logf[:], start=True, stop=True), R=[CSTb, logfb], W=[PAh.b])
                                yield
                                sb.op("act", lambda e: e.activation(out=eng[:], in_=PAh.ap(512), func=AF.Exp), R=[PAh.b], W=[engb])
                                yield
                                sb.op("dve", lambda e: e.tensor_tensor(out=kh[:], in0=kk[:], in1=eng[:], op=ALU.mult), R=[kkb, engb], W=[khb])
                                yield
                                for h in range(4):
                                    sb.op("pe", lambda e: e.transpose(PTh[:, h * 128:(h + 1) * 128], kt[:, h * 128:(h + 1) * 128], IDB[:]),
                                          R=[ktb, IDBb], W=[PThb], inc=(h == 3))
                                yield
                                sb.op("dve", lambda e: e.tensor_copy(out=f2(ktt[:]), in_=PTh), R=[PThb], W=[kttb])
                                yield
                                for h in range(4):
                                    sb.op("pe", lambda e: e.matmul(PBh.ap(128, h * 128), lhsT=logf[:, h * 128:(h + 1) * 128], rhs=AMID[d], start=True, stop=True),
                                          R=[CSTb, logfb], W=[PBh.b], inc=(h == 3))
                                yield
                                sb.op("act", lambda e: e.activation(out=f2(egt[:]), in_=PBh.ap(512), func=AF.Exp), R=[PBh.b], W=[egtb])
                                yield
                                sb.op("dve", lambda e: e.tensor_tensor(out=qtl[:], in0=Q[:, hs:hs + 4, tsl], in1=egt[:], op=ALU.mult), R=[Qb, egtb], W=[qtlb])
                                yield
                                for h in range(4):
                                    sb.op("pe", lambda e: e.matmul(PBh.ap(128, h * 128), lhsT=logf[:, h * 128:(h + 1) * 128], rhs=UBD[d], start=True, stop=True),
                                          R=[CSTb, logfb], W=[PBh.b], inc=(h == 3))
                                yield
                                sb.op("act", lambda e: e.activation(out=f2(eg[:]), in_=PBh.ap(512), func=AF.Exp), R=[PBh.b], W=[egb])
                                yield
                                sb.op("dve", lambda e: e.tensor_tensor(out=qts[:], in0=Q[:, hs:hs + 4, tsl], in1=eg[:], op=ALU.mult), R=[Qb, egb], W=[qtsb])
                                yield
                                for h in range(4):
                                    if gi_ > 0:
                                        sb.op("pe", lambda e: e.matmul(PCh.ap(128, h * 128), lhsT=car[:, h * 128:(h + 1) * 128], rhs=ONES, start=True, stop=False),
                                              R=[CSTb, carb], W=[PCh.b], inc=False)
                                    sb.op("pe", lambda e: e.matmul(PCh.ap(128, h * 128), lhsT=logf[:, h * 128:(h + 1) * 128], rhs=UTR[d], start=(gi_ == 0), stop=True),
                                          R=[CSTb, logfb], W=[PCh.b], inc=(h == 3))
                                yield
                                sb.op("act", lambda e: e.activation(out=f2(egt[:]), in_=PCh.ap(512), func=AF.Exp), R=[PCh.b], W=[egtb])
                                yield
                                sb.op("dve", lambda e: e.tensor_tensor(out=qti[:, :, tsl], in0=Q[:, hs:hs + 4, tsl], in1=egt[:], op=ALU.mult), R=[Qb, egtb], W=[qtib])
                                if gi_ == 0:
                                    sb.op("dve", lambda e: e.tensor_copy(out=car[:], in_=logf[:]), R=[logfb], W=[carb])
                                elif gi_ < NG - 1:
                                    sb.op("dve", lambda e: e.tensor_tensor(out=car[:], in0=car[:], in1=logf[:], op=ALU.add), R=[logfb, carb], W=[carb])
                                if gi_ == NG - 1:
                                    ecol = 127 if d == 0 else 0
                                    sb.op("dve", lambda e: e.tensor_copy(out=dt_[:], in_=egt[:, :, ecol]), R=[egtb], W=[dtb])
                                yield
                                for h in range(4):
                                    sb.op("pe", lambda e: e.matmul(PAh.ap(128, h * 128), lhsT=ktt[:, h, :], rhs=qtl[:, h, :], start=True, stop=True),
                                          R=[kttb, qtlb], W=[PAh.b], inc=(h == 3))
                                yield
                                sb.op("dve", lambda e: e.copy_predicated(out=scm[:], mask=MSK[:].bitcast(mybir.dt.uint32), data=v3(PAh.ap(512))),
                                      R=[PAh.b, MSKb, scmb], W=[scmb])
                                yield
                                for h in range(4):
                                    sb.op("pe", lambda e: e.matmul(PBh.ap(128, h * 128), lhsT=VT[:, tg, (hs + h) * 128:(hs + h + 1) * 128], rhs=scm[:, h, :], start=(h == 0), stop=False),
                                          R=[VTb, scmb], W=[PBh.b], inc=(h == 3))
                                yield
                                dsv = v3(PCh.ap(512))

                                def evj(j):
                                    ce = j * 32 + 31 if d == 0 else j * 32
                                    return eg[:, :, ce:ce + 1].to_broadcast([128, 4, 128])
                                if not first_step:
                                    sb.op("dve", lambda e: e.tensor_tensor(out=tmp[:], in0=s_[:], in1=evj(jorder[0]), op=ALU.mult), R=[s_b, egb], W=[tmpb])
                                for ji, j in enumerate(jorder):
                                    jsl = slice(j * 32, (j + 1) * 32)
                                    for h in range(4):
                                        sb.op("pe", lambda e: e.matmul(PCh.ap(128, h * 128), lhsT=kh[:, h * 128:(h + 1) * 128], rhs=vm[:, j, h * 128:(h + 1) * 128], start=True, stop=True),
                                              R=[khb, vmb], W=[PCh.b], inc=(h == 3))
                                    if not first_step:
                                        for h in range(4):
                                            sb.op("pe", lambda e: e.matmul(PBh.ap(32, h * 128 + j * 32), lhsT=sbf[:, h, :], rhs=qts[:, h, jsl], start=False, stop=True),
                                                  R=[sbfb, qtsb], W=[PBh.b], inc=(h == 3))
                                    yield
                                    if first_step:
                                        sb.op("dve", lambda e: e.tensor_copy(out=sbf[:], in_=dsv), R=[PCh.b], W=[sbfb])
                                        sb.op("dve", lambda e: e.tensor_copy(out=s_[:], in_=dsv), R=[PCh.b], W=[s_b])
                                    else:
                                        sb.op("dve", lambda e: e.tensor_tensor(out=sbf[:], in0=dsv, in1=tmp[:], op=ALU.add), R=[PCh.b, tmpb], W=[sbfb])
                                        sb.op("dve", lambda e: e.tensor_tensor(out=s_[:], in0=dsv, in1=tmp[:], op=ALU.add), R=[PCh.b, tmpb], W=[s_b])
                                    first_step = False
                                    if ji < 3:
                                        sb.op("dve", lambda e: e.tensor_tensor(out=tmp[:], in0=s_[:], in1=evj(jorder[ji + 1]), op=ALU.mult), R=[s_b, egb], W=[tmpb])
                                    yield
                                ov = v3(PBh.ap(512))
                                if d == 0:
                                    sb.op("dve", lambda e: e.tensor_copy(out=OL[:, hs:hs + 4, tsl], in_=ov), R=[PBh.b], W=[olb])
                                else:
                                    sb.op("dve", lambda e: e.tensor_tensor(out=OL[:, hs:hs + 4, tsl], in0=ov, in1=OL[:, hs:hs + 4, tsl], op=ALU.add), R=[PBh.b, olb], W=[olb])
                                yield
                            sb.dma("sp", LST[i][d][0][:, hs:hs + 4, :], s_[:], s_b, LST[i][d][1], s_b)
                            sb.dma("sp", DST[i][d][0][:, hs:hs + 4], dt_[:], dtb, DST[i][d][1], dtb)
                            sb.dma("sp", QTS[i][d][0][:, hs:hs + 4, 0:T], qti[:], qtib, QTS[i][d][1], qtib)

                        gens = [chain(0), chain(1)]
                        alive = [True, True]
                        while any(alive):
                            for gi2 in range(2):
                                if alive[gi2]:
                                    try:
                                        next(gens[gi2])
                                    except StopIteration:
                                        alive[gi2] = False
                for hh in range(2):
                    sb.dma("sp", OLOC[i][0][:, hh * 4:hh * 4 + 4, 0:T], OL[:, hh * 4:hh * 4 + 4, :], OLB[hh], OLOC[i][1], OLB[hh])

        def run_boundary(l):
            with Stage(sb, "bnd") as st:
                SF, SFb = st.tile([128, 8, 128], F32)
                LT, LTb = st.tile([128, 8, 128], F32)
                DTt, DTtb = st.tile([128, 8], F32)
                S0 = [st.tile([128, 8, 128], F32) for _ in range(2)]
                EX = [st.tile([128, 8, 128], F32) for _ in range(2)]
                SB16, SB16b = st.tile([128, 8, 128], BF16)

                def recur(d, init, initb, store):
                    order = list(range(NT)) if d == 0 else list(range(NT - 1, -1, -1))
                    sb.op("dve", lambda e: e.tensor_copy(out=SF[:], in_=init[:]), R=[initb], W=[SFb])
                    for i in order:
                        if store:
                            sb.op("act", lambda e: e.activation(out=SB16[:].rearrange("p h s -> p (h s)"), in_=SF[:].rearrange("p h s -> p (h s)"), func=AF.Identity), R=[SFb], W=[SB16b])
                            sb.dma("sp", SIN[i][d][0], SB16[:], SB16b, SIN[i][d][1], SB16b)
                        sb.dma("sp", LT[:], LST[i][d][0], LST[i][d][1], LTb, LTb)
                        sb.dma("sp", DTt[:], DST[i][d][0], DST[i][d][1], DTtb, DTtb)
                        sb.op("dve", lambda e: e.tensor_tensor(out=SF[:], in0=SF[:], in1=DTt[:].unsqueeze(2).to_broadcast([128, 8, 128]), op=ALU.mult), R=[SFb, DTtb], W=[SFb])
                        sb.op("dve", lambda e: e.tensor_tensor(out=SF[:], in0=SF[:], in1=LT[:], op=ALU.add), R=[SFb, LTb], W=[SFb])

                for d in range(2):
                    sb.dma("sp", S0[d][0][:], LST[CTX][d][0], LST[CTX][d][1], S0[d][1], S0[d][1])
                for d in range(2):
                    recur(d, S0[d][0], S0[d][1], False)
                    sb.dma("sp", CCI[0][d * 128:(d + 1) * 128, :].rearrange("p (h s) -> p h s", h=8), SF[:], SFb, CCI[1], SFb)
                sb.op("pool", lambda e: e.collective_compute("AllGather", ALU.bypass, replica_groups=[[0, 1], [2, 3], [4, 5], [6, 7]],
                                                             ins=[CCI[0].opt()], outs=[CCO[0].opt()]), R=[CCI[1]], W=[CCO[1]])
                sb.dma("sp", EX[0][0][:], CCO[0][0:128, :].rearrange("p (h s) -> p h s", h=8), CCO[1], EX[0][1], EX[0][1])
                sb.dma("sp", EX[1][0][:], CCO[0][384:512, :].rearrange("p (h s) -> p h s", h=8), CCO[1], EX[1][1], EX[1][1])
                for d in range(2):
                    own, ownb = S0[d]
                    oth, othb = EX[d]
                    a, b_ = (0, 1) if d == 0 else (1, 0)
                    sb.op("dve", lambda e: e.tensor_scalar_mul(out=own[:], in0=own[:], scalar1=SEL[:, a:a + 1]), R=[ownb, SELb], W=[ownb])
                    sb.op("dve", lambda e: e.scalar_tensor_tensor(out=own[:], in0=oth[:], scalar=SEL[:, b_:b_ + 1], in1=own[:], op0=ALU.mult, op1=ALU.add),
                          R=[othb, SELb, ownb], W=[ownb])
                    recur(d, own, ownb, True)

        def run_hg2(l, i, ti, HA, HAb, HM, HMb):
            idx = l // 2
            T, s = ti["T"], ti["s"]
            corr = (i != CTX)
            with Stage(sb, "hg2") as st:
                O, Ob = st.tile([128, 8, T], F32)
                GTt, GTb = st.tile([128, 8, T], BF16)
                OSQ, OSQb = st.tile([128, 8, T], F32)
                RS, RSb = st.tile([128, 8, T], F32)
                R, Rb = st.tile([128, 8, T], BF16)
                ln = LN(st, T, HA, HAb, HM, HMb, l, s, 0)
                sb.dma("sp", O[:], OLOC[i][0][:, :, 0:T], OLOC[i][1], Ob, Ob)
                sb.dma("sp", GTt[:], GATE[i][0][:, :, 0:T], GATE[i][1], GTb, GTb)
                if corr:
                    QD = [st.tile([128, 8, T], BF16) for _ in range(2)]
                    SI = [st.tile([128, 8, 128], BF16) for _ in range(2)]
                    for d in range(2):
                        sb.dma("sp", QD[d][0][:], QTS[i][d][0][:, :, 0:T], QTS[i][d][1], QD[d][1], QD[d][1])
                        sb.dma("sp", SI[d][0][:], SIN[i][d][0], SIN[i][d][1], SI[d][1], SI[d][1])
                    for h in range(8):
                        bank = YB[h % 4]
                        for d in range(2):
                            sb.op("pe", lambda e: e.matmul(bank.ap(T), lhsT=SI[d][0][:, h, :], rhs=QD[d][0][:, h, :], start=(d == 0), stop=(d == 1)),
                                  R=[SI[d][1], QD[d][1]], W=[bank.b], inc=(d == 1))
                        sb.op("dve", lambda e: e.tensor_tensor(out=O[:, h, :], in0=bank.ap(T), in1=O[:, h, :], op=ALU.add), R=[bank.b, Ob], W=[Ob])
                for h in range(8):
                    sb.op("act", lambda e: e.activation(out=OSQ[:, h, :], in_=O[:, h, :], func=AF.Square), R=[Ob], W=[OSQb])
                    bank = YB[h % 4]
                    sb.op("pe", lambda e: e.matmul(bank.ap(T), lhsT=ONESR[:], rhs=OSQ[:, h, :], start=True, stop=True), R=[ONESRb, OSQb], W=[bank.b])
                    sb.op("act", lambda e: e.activation(out=RS[:, h, :], in_=bank.ap(T), func=AF.Sqrt, bias=RMS_EPS), R=[bank.b], W=[RSb])
                    sb.op("dve", lambda e: e.reciprocal(out=RS[:, h, :], in_=RS[:, h, :]), R=[RSb], W=[RSb])
                    sb.op("dve", lambda e: e.tensor_tensor(out=O[:, h, :], in0=O[:, h, :], in1=RS[:, h, :], op=ALU.mult), R=[Ob, RSb], W=[Ob])
                    sb.op("dve", lambda e: e.scalar_tensor_tensor(out=R[:, h, :], in0=O[:, h, :], scalar=vcol(("nw", idx)), in1=GTt[:, h, :], op0=ALU.mult, op1=ALU.mult),
                          R=[Ob, VECb, GTb], W=[Rb])
                ybi = 0
                for g in range(2):
                    slot = ws.next()
                    w = wview(slot, 0, 8, 512)
                    for f in range(4):
                        oc = 4 * g + f
                        yb = YB[ybi % 4]
                        ybi += 1
                        for kc in range(8):
                            sb.op("pe", lambda e: e.matmul(yb.ap(T), lhsT=w[:, kc, f * 128:(f + 1) * 128], rhs=R[:, kc, :], start=(kc == 0), stop=(kc == 7)),
                                  R=[slot[1], Rb], W=[yb.b], inc=(kc == 7))
                        ln.accum(oc, yb)
                ln.finish()

        sched = []

        def ctx_needed_in_loop(l):
            return l + 2 < DEPTH

        l = 0
        for i in [CTX] + list(range(NT)):
            sched.append(("in0", i))
        while l < DEPTH:
            sched.append(("bnd", l))
            tl = ([CTX] if ctx_needed_in_loop(l) else []) + list(range(NT))
            for i in tl:
                sched.append(("tile", l, i))
            l += 2

        for it in sched:
            if it[0] == "in0":
                plan_hg1(0)
            elif it[0] == "tile":
                l = it[1]
                plan_hg2(l // 2)
                plan_ffn(l)
                if l + 1 < DEPTH:
                    plan_sgu((l + 1) // 2)
                    plan_ffn(l + 1)
                if l + 2 < DEPTH:
                    plan_hg1((l + 2) // 2)

        def load_x(ti, X, Xb):
            sb.dma("sp", X[:], ti["src"], WD, Xb, Xb)

        for it in sched:
            if it[0] == "in0":
                i = it[1]
                ti = tinfo(i)
                T, s = ti["T"], ti["s"]
                with Stage(sb, "t0") as tsg:
                    X, Xb = tsg.tile([128, 8, T], F32)
                    HM, HMb0 = tsg.tile([128, 8, T], BF16)
                    HMb = [HMb0] + [Buf("hmc") for _ in range(7)]
                    tsg.bufs.extend(HMb[1:])
                    load_x(ti, X, Xb)
                    for c in range(8):
                        sb.op("act", lambda e: e.activation(out=HM[:, c, :], in_=X[:, c, :], func=AF.Identity, scale=der(0, s, 0, c), bias=der(0, s, 1, c)),
                              R=[Xb, DERb], W=[HMb[c]])
                    run_hg1(0, i, ti, HM, HMb)
            elif it[0] == "bnd":
                run_boundary(it[1])
            else:
                l, i = it[1], it[2]
                ti = tinfo(i)
                T, s = ti["T"], ti["s"]
                with Stage(sb, "tl") as tsg:
                    HA, HAb = tsg.tile([128, 8, T], F32)
                    HM, HMb0 = tsg.tile([128, 8, T], BF16)
                    HMb = [HMb0] + [Buf("hmc") for _ in range(7)]
                    tsg.bufs.extend(HMb[1:])
                    if l == 0:
                        load_x(ti, HA, HAb)
                        sb.op("act", lambda e: e.activation(out=HA[:].rearrange("p c t -> p (c t)"), in_=HA[:].rearrange("p c t -> p (c t)"), func=AF.Identity, scale=ALPHA),
                              R=[HAb], W=[HAb])
                    else:
                        sb.dma("sp", HA[:], HSCR[i][0], HSCR[i][1], HAb, HAb)
                    run_hg2(l, i, ti, HA, HAb, HM, HMb)
                    dump('h1_%d_%d' % (l, i), HA[:], HAb, [128, 8, T])
                    dump('hm1_%d_%d' % (l, i), HM[:], HMb[7], [128, 8, T], BF16)
                    run_ffn(l, ti, HA, HAb, HM, HMb)
                    dump('h2_%d_%d' % (l, i), HA[:], HAb, [128, 8, T])
                    if l + 1 < DEPTH:
                        run_sgu(l + 1, ti, HA, HAb, HM, HMb)
                        dump('h3_%d_%d' % (l, i), HA[:], HAb, [128, 8, T])
                        run_ffn(l + 1, ti, HA, HAb, HM, HMb)
                        dump('h4_%d_%d' % (l, i), HA[:], HAb, [128, 8, T])
                    if l + 2 < DEPTH:
                        if i != CTX:
                            sb.dma("sp", HSCR[i][0], HA[:], HAb, HSCR[i][1], HAb)
                        run_hg1(l + 2, i, ti, HM, HMb)
                    elif i != CTX:
                        sb.dma("sp", outT[:, :, i * TT:(i + 1) * TT], HA[:], HAb, OUTB, HAb)
        sb.wait_all("sp", [OUTB, DBGB])
        assert ws.taken == len(ws.plan), (ws.taken, len(ws.plan))
    return nc, sb


def _cols(v):
    v = np.asarray(v, np.float32).reshape(-1, 128)
    return np.ascontiguousarray(v.T)


def make_consts():
    s = np.arange(128)[:, None]
    t = np.arange(128)[None, :]
    same = (s // 32) == (t // 32)
    c = np.zeros((128, NCST), np.float32)
    c[:, 0:128] = np.eye(128)
    c[:, 128:256] = same & (s <= t)
    c[:, 256:384] = same & (s >= t)
    c[:, 384:512] = (s <= t)
    c[:, 512:640] = (s >= t)
    c[:, 640:768] = 1.0
    for j in range(4):
        c[32 * j:32 * (j + 1), 768 + j] = 1.0
    for d in range(2):
        pos = (lambda x: x % 32) if d == 0 else (lambda x: 31 - (x % 32))
        pu, pt = pos(s), pos(t)
        A = np.where((pu > 15) & (pu <= pt), 1.0, 0.0) - np.where((pu > pt) & (pu <= 15), 1.0, 0.0)
        c[:, 772 + 128 * d:900 + 128 * d] = A * same
        c[:, 1028 + 128 * d:1156 + 128 * d] = ((pu > pt) & same)
    return c


def make_inputs(inp, NT, DEPTH):
    NH, NS = (DEPTH + 1) // 2, max(DEPTH // 2, 1)
    VOFF, NV = vec_layout(DEPTH)
    f = lambda k: np.asarray(inp[k], np.float32)
    vec = np.zeros((128, NV), np.float32)

    def put(key, v):
        c = _cols(v)
        vec[:, VOFF[key]:VOFF[key] + c.shape[1]] = c
    for l in range(DEPTH):
        put(("adab", l), f("ada_b")[l])
        for w in range(2):
            put(("lng", l, w), f("ln_g")[l, w])
            put(("lnb", l, w), f("ln_b")[l, w])
        for k in range(3):
            put(("cw", l, k), f("ffn_conv_w")[l, k])
        put(("cb", l), f("ffn_conv_b")[l])
    for i in range(NH):
        put(("nw", i), f("hg_norm_w")[i])
    for i in range(min(NS, DEPTH // 2)):
        put(("sg", i), f("sgu_ln_g")[i])
        put(("sb", i), f("sgu_ln_b")[i])
    shared = dict(
        ada_w=np.ascontiguousarray(f("ada_w")[:DEPTH]), hg_w_in=np.ascontiguousarray(f("hg_w_in")[:NH]),
        hg_w_out=np.ascontiguousarray(f("hg_w_out")[:NH]), sgu_w_in=np.ascontiguousarray(f("sgu_w_in")[:NS]),
        sgu_w_out=np.ascontiguousarray(f("sgu_w_out")[:NS]), ffn_w_up=np.ascontiguousarray(f("ffn_w_up")[:DEPTH]),
        ffn_w_down=np.ascontiguousarray(f("ffn_w_down")[:DEPTH]), vecs=vec, hg_lb=f("hg_lb"),
        sgu_bs=np.ascontiguousarray(f("sgu_b_s")[:NS]),
        sgu_wsT=np.ascontiguousarray(f("sgu_w_s")[:NS].transpose(0, 3, 1, 2)),
        consts=make_consts())
    x, c, ctx, c_ctx = f("x"), f("c"), f("ctx"), f("c_ctx")
    NTT = NT * TT
    maps = []
    for core in range(8):
        b, half = core // 2, core % 2
        xs = x[b, half * NTT:(half + 1) * NTT, :]
        xTc = np.ascontiguousarray(xs.T.reshape(8, 128, NTT).transpose(1, 0, 2))
        cT = np.ascontiguousarray(ctx[b].T.reshape(8, 128, CT).transpose(1, 0, 2))
        cv = np.stack([_cols(c[b]), _cols(c_ctx)], axis=2)
        se = np.zeros((128, 2), np.float32)
        se[:, half] = 1.0
        m = dict(shared)
        m.update(xT=xTc, ctxT=cT, cvec=np.ascontiguousarray(cv), sel=se)
        maps.append(m)
    return maps


_CACHE = {}


def run(inp, NT, DEPTH):
    key = (NT, DEPTH)
    if key not in _CACHE:
        _CACHE[key] = build(NT, DEPTH)[0]
    nc = _CACHE[key]
    maps = make_inputs(inp, NT, DEPTH)
    res = run_bass_kernel_spmd(nc, maps, core_ids=list(range(8)))
    global LAST
    LAST = res.results
    NTT = NT * TT
    B = 4
    out = np.zeros((B, 2 * NTT, 1024), np.float32)
    for core in range(8):
        b, half = core // 2, core % 2
        o = res.results[core]["outT"]
        out[b, half * NTT:(half + 1) * NTT, :] = o.transpose(2, 1, 0).reshape(NTT, 1024)
    return out


def kernel(**inputs):
    x = np.asarray(inputs["x"])
    NT = x.shape[1] // (2 * TT)
    return run(inputs, NT, 4)
```

```python
import numpy as np
from contextlib import ExitStack
import concourse.bass as bass
import concourse.mybir as mybir
from concourse.bass_utils import run_bass_kernel_spmd

F32 = mybir.dt.float32
BF16 = mybir.dt.bfloat16
AF = mybir.ActivationFunctionType
ALU = mybir.AluOpType

ALPHA = 8.0 ** 0.25
LN_EPS = 1e-5
RMS_EPS = 1e-6
TT = 512
CT = 256
NCST = 10 * 128 + 4
class Buf:
    __slots__ = ("name", "w", "r", "dsem", "dval")

    def __init__(self, name):
        self.name = name
        self.w = {}
        self.r = {}
        self.dsem = None
        self.dval = 0


class SB:
    COMPUTE = ("pe", "act", "dve", "pool")
    NOBARRIER = ("pe", "pool")

    def __init__(self, nc):
        self.nc = nc
        self.E = {"pe": nc.tensor, "act": nc.scalar, "dve": nc.vector,
                  "pool": nc.gpsimd, "sp": nc.sync}
        self.sem = {k: nc.alloc_semaphore("sem_" + k) for k in self.E}
        self.cnt = {k: 0 for k in self.E}
        self.seen = {k: {} for k in self.E}
        self.pending = {}
        self.sem_pool = []
        self.nsem = 0
        self.ninst = 0

    @staticmethod
    def _merge(dst, deps):
        for key, d in deps.items():
            if key not in dst or dst[key][1] < d[1]:
                dst[key] = d

    def _wait(self, k, deps, raw):
        for key, (s, v, ek) in deps.items():
            if ek == k:
                if k == "pe" or not raw:
                    continue
            if self.seen[k].get(key, 0) < v:
                self.E[k].wait_ge(s, v)
                self.seen[k][key] = v

    def op(self, k, fn, R=(), W=(), inc=True):
        draw = {}
        doth = {}
        for b in R:
            self._merge(draw, b.w)
        for b in W:
            self._merge(doth, b.w)
            self._merge(doth, b.r)
        self._wait(k, draw, True)
        self._wait(k, doth, False)
        ins = fn(self.E[k])
        self.ninst += 1
        if inc:
            self.cnt[k] += 1
            ins.then_inc(self.sem[k], 1)
            v = self.cnt[k]
        else:
            v = self.cnt[k] + 1
        d = (self.sem[k], v, k)
        key = "e_" + k
        for b in R:
            b.r[key] = d
        for b in W:
            b.w[key] = d
        return ins

    def _dsem(self, b):
        if b.dsem is None:
            if self.sem_pool:
                b.dsem, b.dval = self.sem_pool.pop()
            else:
                self.nsem += 1
                b.dsem = self.nc.alloc_semaphore("dsem%d" % self.nsem)
                b.dval = 0
        return b.dsem

    def dma(self, q, out_ap, in_ap, src, dst, owner):
        deps = {}
        self._merge(deps, src.w)
        self._merge(deps, dst.w)
        self._merge(deps, dst.r)
        for key, (s, v, ek) in deps.items():
            if self.seen[q].get(key, 0) < v:
                self.E[q].wait_ge(s, v)
                self.seen[q][key] = v
        sem = self._dsem(owner)
        owner.dval += 16
        ins = self.E[q].dma_start(out=out_ap, in_=in_ap)
        ins.then_inc(sem, 16)
        self.ninst += 1
        d = (sem, owner.dval, None)
        key = "d_%d" % id(sem)
        src.r[key] = d
        dst.w[key] = d
        return ins

    def release(self, bufs):
        for b in bufs:
            self._merge(self.pending, b.w)
            self._merge(self.pending, b.r)
            if b.dsem is not None:
                self.sem_pool.append((b.dsem, b.dval))
                b.dsem = None

    def barrier(self):
        deps = dict(self.pending)
        for k in self.COMPUTE:
            deps["e_" + k] = (self.sem[k], self.cnt[k], k)
        for k in self.E:
            if k in self.NOBARRIER:
                continue
            for key, (s, v, ek) in deps.items():
                if ek == k:
                    continue
                if self.seen[k].get(key, 0) < v:
                    self.E[k].wait_ge(s, v)
                    self.seen[k][key] = v
        self.pending = {}

    def wait_all(self, k, bufs):
        deps = {}
        for b in bufs:
            self._merge(deps, b.w)
            self._merge(deps, b.r)
        for key, (s, v, ek) in deps.items():
            if self.seen[k].get(key, 0) < v:
                self.E[k].wait_ge(s, v)
                self.seen[k][key] = v


class Stage:
    UID = 0

    def __init__(self, sb, name):
        self.sb = sb
        self.name = name
        self.stk = ExitStack()
        self.bufs = []
        self.n = 0

    def __enter__(self):
        self.stk.__enter__()
        return self

    def tile(self, shape, dtype, name=None):
        Stage.UID += 1
        nm = "%s_%s%d" % (self.name, name or "t", Stage.UID)
        t = self.stk.enter_context(self.sb.nc.sbuf_tensor(nm, list(shape), dtype))
        b = Buf(nm)
        self.bufs.append(b)
        return t, b

    def __exit__(self, *a):
        self.sb.release(self.bufs)
        self.sb.barrier()
        return self.stk.__exit__(*a)


class Bank:
    def __init__(self, t, lo, b):
        self.t, self.lo, self.b = t, lo, b

    def ap(self, n=512, off=0):
        return self.t[:, self.lo + off:self.lo + off + n]


class WStream:
    def __init__(self, sb, slots):
        self.sb, self.slots = sb, slots
        self.plan = []
        self.issued = 0
        self.taken = 0

    def add(self, pieces):
        self.plan.append(pieces)

    def next(self, keep=0):
        n = self.taken
        self.taken += 1
        ns = len(self.slots)
        while self.issued < min(len(self.plan), n + ns - keep):
            t, b = self.slots[self.issued % ns]
            for (src, off, kc, ncol, db) in self.plan[self.issued]:
                dst = t[:, off:off + kc * ncol].rearrange("p (k n) -> p k n", k=kc)
                self.sb.dma("pool", dst, src, db, b, b)
            self.issued += 1
        return self.slots[n % ns]


def vec_layout(DEPTH):
    NH, NS = (DEPTH + 1) // 2, max(DEPTH // 2, 1)
    off = {}
    n = 0

    def add(key, cnt):
        nonlocal n
        off[key] = n
        n += cnt
    for l in range(DEPTH):
        add(("adab", l), 48)
        for w in range(2):
            add(("lng", l, w), 8)
            add(("lnb", l, w), 8)
        for k in range(3):
            add(("cw", l, k), 44)
        add(("cb", l), 44)
    for i in range(NH):
        add(("nw", i), 1)
    for i in range(NS):
        add(("sg", i), 24)
        add(("sb", i), 24)
    return off, n


DEBUG = False


def build(NT, DEPTH):
    NH, NS = (DEPTH + 1) // 2, max(DEPTH // 2, 1)
    VOFF, NV = vec_layout(DEPTH)
    NTT = NT * TT
    nc = bass.Bass("TRN2", target_bir_lowering=False)
    sb = SB(nc)

    def din(name, shape):
        return nc.dram_tensor(name, list(shape), F32, kind="ExternalInput").ap()

    xT = din("xT", [128, 8, NTT])
    ctxT = din("ctxT", [128, 8, CT])
    cvec = din("cvec", [128, 8, 2])
    sel = din("sel", [128, 2])
    ada_w = din("ada_w", [DEPTH, 1024, 6144])
    hg_w_in = din("hg_w_in", [NH, 1024, 5120])
    hg_w_out = din("hg_w_out", [NH, 1024, 1024])
    sgu_w_in = din("sgu_w_in", [NS, 1024, 6144])
    sgu_w_out = din("sgu_w_out", [NS, 3072, 1024])
    ffn_w_up = din("ffn_w_up", [DEPTH, 1024, 5632])
    ffn_w_down = din("ffn_w_down", [DEPTH, 2816, 1024])
    vecs = din("vecs", [128, NV])
    hg_lb = din("hg_lb", [2, 2, 1024])
    sgu_bs = din("sgu_bs", [NS, 8, 128])
    sgu_wsT = din("sgu_wsT", [NS, 128, 8, 128])
    consts = din("consts", [128, NCST])
    outT = nc.dram_tensor("outT", [128, 8, NTT], F32, kind="ExternalOutput").ap()

    WD = Buf("wdram")
    OUTB = Buf("outT")

    NTILE = NT + 1
    CTX = NT

    def tinfo(i):
        if i == CTX:
            return dict(T=CT, s=1, rl=CT, src=ctxT)
        return dict(T=TT, s=0, rl=64, src=xT[:, :, i * TT:(i + 1) * TT])

    def dscr(name, shape, dt):
        if DEBUG and not name.startswith("cc_"):
            return nc.dram_tensor(name, list(shape), dt, kind="ExternalOutput").ap(), Buf(name)
        return nc.dram_tensor(name, list(shape), dt).ap(), Buf(name)

    DBGB = Buf("dbg")

    def dump(name, t, tb, shape, dt=F32):
        if not DEBUG:
            return
        d = nc.dram_tensor("dbg_" + name, list(shape), dt, kind="ExternalOutput").ap()
        sb.dma("sp", d, t, tb, DBGB, tb)

    HSCR = [dscr("hscr%d" % i, [128, 8, TT], F32) for i in range(NT)]
    OLOC = [dscr("oloc%d" % i, [128, 8, TT], F32) for i in range(NTILE)]
    QTS = [[dscr("qts%d_%d" % (i, d), [128, 8, TT], BF16) for d in range(2)] for i in range(NTILE)]
    GATE = [dscr("gate%d" % i, [128, 8, TT], BF16) for i in range(NTILE)]
    LST = [[dscr("lst%d_%d" % (i, d), [128, 8, 128], F32) for d in range(2)] for i in range(NTILE)]
    DST = [[dscr("dst%d_%d" % (i, d), [128, 8], F32) for d in range(2)] for i in range(NTILE)]
    SIN = [[dscr("sin%d_%d" % (i, d), [128, 8, 128], BF16) for d in range(2)] for i in range(NT)]
    CCI = dscr("cc_in", [256, 1024], F32)
    CCO = dscr("cc_out", [512, 1024], F32)

    res = ExitStack()
    with res:
        def rtile(name, shape, dt):
            t = res.enter_context(nc.sbuf_tensor(name, list(shape), dt))
            return t, Buf(name)

        VEC, VECb = rtile("VEC", [128, NV], F32)
        CST, CSTb = rtile("CST", [128, NCST], F32)
        IDB, IDBb = rtile("IDB", [128, 128], BF16)
        ONESM, ONESMb = rtile("ONESM", [128, 128], F32)
        ONESR, ONESRb = rtile("ONESR", [128, 128], F32)
        MOD, MODb = rtile("MOD", [128, DEPTH, 48, 2], F32)
        DER, DERb = rtile("DER", [128, DEPTH, 2, 12, 8], F32)
        SEL, SELb = rtile("SEL", [128, 2], F32)
        slots = [rtile("wslot%d" % i, [128, 6144], BF16) for i in range(4)]
        for _t, _b in slots:
            sb._dsem(_b)
        ws = WStream(sb, slots)

        PAt = res.enter_context(nc.psum_tensor("PA", [128, 1024], F32))
        PBt = res.enter_context(nc.psum_tensor("PB", [128, 1024], F32))
        PCt = res.enter_context(nc.psum_tensor("PC", [128, 1024], F32))
        PDt = res.enter_context(nc.psum_tensor("PD", [128, 512], F32))
        PTt = res.enter_context(nc.psum_tensor("PT", [128, 1024], BF16))
        PA0, PA1 = Bank(PAt, 0, Buf("PA0")), Bank(PAt, 512, Buf("PA1"))
        PB0, PB1 = Bank(PBt, 0, Buf("PB0")), Bank(PBt, 512, Buf("PB1"))
        PC0, PC1 = Bank(PCt, 0, Buf("PC0")), Bank(PCt, 512, Buf("PC1"))
        PD = Bank(PDt, 0, Buf("PD"))
        PTb = Buf("PT")
        PTb2 = Buf("PT2")

        I_f = CST[:, 0:128]
        UBD = [CST[:, 128:256], CST[:, 256:384]]
        UTR = [CST[:, 384:512], CST[:, 512:640]]
        ONES = CST[:, 640:768]
        RMASK = CST[:, 768:772]
        AMID = [CST[:, 772:900], CST[:, 900:1028]]
        SUPM = [CST[:, 1028:1156], CST[:, 1156:1284]]

        def vcol(key, j=0, n=1):
            o = VOFF[key] + j
            return VEC[:, o:o + n]

        def der(l, s, k, c=None):
            if c is None:
                return DER[:, l, s, k, :]
            return DER[:, l, s, k, c:c + 1]

        sb.dma("sp", VEC[:], vecs, WD, VECb, VECb)
        sb.dma("sp", CST[:], consts, WD, CSTb, CSTb)
        sb.dma("sp", SEL[:], sel, WD, SELb, SELb)
        sb.op("dve", lambda e: e.tensor_copy(out=IDB[:], in_=I_f), R=[CSTb], W=[IDBb])
        sb.op("dve", lambda e: e.memset(ONESM[:], 1.0 / 1024.0), W=[ONESMb])
        sb.op("dve", lambda e: e.memset(ONESR[:], 1.0 / 128.0), W=[ONESRb])

        with Stage(sb, "mod") as st:
            CV, CVb = st.tile([128, 8, 2], F32)
            SCV, SCVb = st.tile([128, 8, 2], F32)
            AW = [st.tile([128, 8, 512], F32) for _ in range(2)]
            sb.dma("sp", CV[:], cvec, WD, CVb, CVb)
            sb.op("act", lambda e: e.activation(out=SCV[:], in_=CV[:], func=AF.Silu), R=[CVb], W=[SCVb])
            gi = 0
            for l in range(DEPTH):
                for g in range(12):
                    A, Ab = AW[gi % 2]
                    gi += 1
                    sb.dma("sp", A[:], ada_w[l, :, g * 512:(g + 1) * 512].rearrange("(k p) n -> p k n", p=128), WD, Ab, Ab)
                    for f in range(4):
                        fc = g * 4 + f
                        for kc in range(8):
                            sb.op("pe", lambda e, A=A, f=f, kc=kc, fc=fc: e.matmul(
                                PD.ap(2, fc * 2), lhsT=A[:, kc, f * 128:(f + 1) * 128], rhs=SCV[:, kc, :],
                                start=(kc == 0), stop=(kc == 7)), R=[Ab, SCVb], W=[PD.b], inc=(kc == 7))
                sb.op("dve", lambda e, l=l: e.tensor_tensor(
                    out=MOD[:, l, :, :], in0=PD.ap(96).rearrange("p (c s) -> p c s", s=2),
                    in1=vcol(("adab", l), 0, 48).unsqueeze(2).to_broadcast([128, 48, 2]), op=ALU.add),
                    R=[PD.b, VECb], W=[MODb])
            for l in range(DEPTH):
                last = (l == DEPTH - 1)
                a2 = 1.0 if last else ALPHA
                for s in range(2):
                    def m(j):
                        return MOD[:, l, j * 8:(j + 1) * 8, s]

                    def mn(j):
                        return MOD[:, l + 1, j * 8:(j + 1) * 8, s]
                    W_ = [DERb]
                    R_ = [MODb, VECb, DERb]
                    sb.op("dve", lambda e: e.tensor_scalar_add(out=der(l, s, 0), in0=m(1), scalar1=1.0), R=R_, W=W_)
                    sb.op("dve", lambda e: e.tensor_copy(out=der(l, s, 1), in_=m(0)), R=R_, W=W_)
                    sb.op("dve", lambda e: e.tensor_copy(out=der(l, s, 2), in_=m(2)), R=R_, W=W_)
                    sb.op("dve", lambda e: e.tensor_copy(out=der(l, s, 3), in_=m(5)), R=R_, W=W_)
                    g1, b1 = vcol(("lng", l, 0), 0, 8), vcol(("lnb", l, 0), 0, 8)
                    g2, b2 = vcol(("lng", l, 1), 0, 8), vcol(("lnb", l, 1), 0, 8)
                    sb.op("dve", lambda e: e.tensor_scalar_mul(out=der(l, s, 4), in0=g1, scalar1=ALPHA), R=R_, W=W_)
                    sb.op("dve", lambda e: e.tensor_scalar_mul(out=der(l, s, 5), in0=b1, scalar1=ALPHA), R=R_, W=W_)
                    sb.op("dve", lambda e: e.scalar_tensor_tensor(out=der(l, s, 6), in0=m(4), scalar=1.0, in1=g1, op0=ALU.add, op1=ALU.mult), R=R_, W=W_)
                    sb.op("dve", lambda e: e.scalar_tensor_tensor(out=der(l, s, 7), in0=m(4), scalar=1.0, in1=b1, op0=ALU.add, op1=ALU.mult), R=R_, W=W_)
                    sb.op("dve", lambda e: e.tensor_tensor(out=der(l, s, 7), in0=der(l, s, 7), in1=m(3), op=ALU.add), R=R_, W=W_)
                    sb.op("dve", lambda e: e.tensor_scalar_mul(out=der(l, s, 8), in0=g2, scalar1=a2), R=R_, W=W_)
                    sb.op("dve", lambda e: e.tensor_scalar_mul(out=der(l, s, 9), in0=b2, scalar1=a2), R=R_, W=W_)
                    if not last:
                        sb.op("dve", lambda e: e.scalar_tensor_tensor(out=der(l, s, 10), in0=mn(1), scalar=1.0, in1=g2, op0=ALU.add, op1=ALU.mult), R=R_, W=W_)
                        sb.op("dve", lambda e: e.scalar_tensor_tensor(out=der(l, s, 11), in0=mn(1), scalar=1.0, in1=b2, op0=ALU.add, op1=ALU.mult), R=R_, W=W_)
                        sb.op("dve", lambda e: e.tensor_tensor(out=der(l, s, 11), in0=der(l, s, 11), in1=mn(0), op=ALU.add), R=R_, W=W_)

        dump('MOD', MOD[:], MODb, [128, DEPTH, 48, 2])
        dump('DER', DER[:], DERb, [128, DEPTH, 2, 12, 8])
        def wsrc(w, c0, n):
            return w[:, c0:c0 + n].rearrange("(k p) n -> p k n", p=128)

        def plan_hg1(idx):
            for g in range(10):
                ws.add([(wsrc(hg_w_in[idx], g * 512, 512), 0, 8, 512, WD)])

        def plan_hg2(idx):
            for g in range(2):
                ws.add([(wsrc(hg_w_out[idx], g * 512, 512), 0, 8, 512, WD)])

        def plan_ffn(l):
            for g in range(11):
                ws.add([(wsrc(ffn_w_up[l], g * 256, 256), 0, 8, 256, WD),
                        (wsrc(ffn_w_up[l], 2816 + g * 256, 256), 2048, 8, 256, WD)])
            for g in range(4):
                ws.add([(wsrc(ffn_w_down[l], g * 256, 256), 0, 22, 256, WD)])

        def plan_sgu(idx):
            for g in range(6):
                ws.add([(wsrc(sgu_w_in[idx], 3072 + g * 512, 512), 0, 8, 512, WD)])
            for g in range(6):
                ws.add([(wsrc(sgu_w_in[idx], g * 512, 512), 0, 8, 512, WD)])
            for g in range(4):
                ws.add([(wsrc(sgu_w_out[idx], g * 256, 256), 0, 24, 256, WD)])

        def wview(slot, off, kc, ncol):
            return slot[0][:, off:off + kc * ncol].rearrange("p (k n) -> p k n", k=kc)

        class LN:
            def __init__(self, st, T, HA, HAb, HM, HMb, l, s, which, dve_sum=False):
                self.T, self.HA, self.HAb, self.HM, self.HMb = T, HA, HAb, HM, HMb
                self.l, self.s = l, s
                self.kga = 2 if which == 0 else 3
                self.kb = 4 if which == 0 else 8
                self.ZSQ = [st.tile([128, T], F32) for _ in range(2)]
                self.MS = st.tile([128, T], F32)
                self.RS = st.tile([128, T], F32)
                self.XC = [st.tile([128, T], F32) for _ in range(2)]
                self.has_hm = not (which == 1 and l == DEPTH - 1)
                self.mean, self.ex2 = PC0, PC1
                self.dve_sum = dve_sum
                if dve_sum:
                    self.ZS = st.tile([128, T], F32)
                    self.ZQ = st.tile([128, T], F32)

            def accum(self, c, ybank):
                T = self.T
                HA, HAb = self.HA, self.HAb
                sb.op("dve", lambda e: e.scalar_tensor_tensor(
                    out=HA[:, c, :], in0=ybank.ap(T), scalar=der(self.l, self.s, self.kga, c), in1=HA[:, c, :],
                    op0=ALU.mult, op1=ALU.add), R=[ybank.b, DERb, HAb], W=[HAb])
                if self.dve_sum:
                    ZS, ZSb = self.ZS
                    ZQ, ZQb = self.ZQ
                    if c == 0:
                        sb.op("act", lambda e: e.activation(out=ZQ[:], in_=HA[:, c, :], func=AF.Square), R=[HAb], W=[ZQb])
                    else:
                        Z, Zb = self.ZSQ[c % 2]
                        sb.op("act", lambda e: e.activation(out=Z[:], in_=HA[:, c, :], func=AF.Square), R=[HAb], W=[Zb])
                        sb.op("dve", lambda e: e.tensor_tensor(out=ZQ[:], in0=ZQ[:], in1=Z[:], op=ALU.add), R=[ZQb, Zb], W=[ZQb])
                        if c == 1:
                            sb.op("dve", lambda e: e.tensor_tensor(out=ZS[:], in0=HA[:, 0, :], in1=HA[:, 1, :], op=ALU.add), R=[HAb], W=[ZSb])
                        else:
                            sb.op("dve", lambda e: e.tensor_tensor(out=ZS[:], in0=ZS[:], in1=HA[:, c, :], op=ALU.add), R=[ZSb, HAb], W=[ZSb])
                    if c == 7:
                        sb.op("pe", lambda e: e.matmul(self.mean.ap(T), lhsT=ONESM[:], rhs=ZS[:], start=True, stop=True), R=[ONESMb, ZSb], W=[self.mean.b])
                        sb.op("pe", lambda e: e.matmul(self.ex2.ap(T), lhsT=ONESM[:], rhs=ZQ[:], start=True, stop=True), R=[ONESMb, ZQb], W=[self.ex2.b])
                    return
                Z, Zb = self.ZSQ[c % 2]
                sb.op("act", lambda e: e.activation(out=Z[:], in_=HA[:, c, :], func=AF.Square), R=[HAb], W=[Zb])
                sb.op("pe", lambda e: e.matmul(self.mean.ap(T), lhsT=ONESM[:], rhs=HA[:, c, :], start=(c == 0), stop=(c == 7)),
                      R=[ONESMb, HAb], W=[self.mean.b], inc=(c == 7))
                sb.op("pe", lambda e: e.matmul(self.ex2.ap(T), lhsT=ONESM[:], rhs=Z[:], start=(c == 0), stop=(c == 7)),
                      R=[ONESMb, Zb], W=[self.ex2.b], inc=True)

            def finish(self):
                T = self.T
                HA, HAb, HM, HMb = self.HA, self.HAb, self.HM, self.HMb
                MS, MSb = self.MS
                RS, RSb = self.RS
                sb.op("act", lambda e: e.activation(out=MS[:], in_=self.mean.ap(T), func=AF.Square), R=[self.mean.b], W=[MSb])
                sb.op("dve", lambda e: e.tensor_tensor(out=RS[:], in0=self.ex2.ap(T), in1=MS[:], op=ALU.subtract), R=[self.ex2.b, MSb], W=[RSb])
                sb.op("act", lambda e: e.activation(out=RS[:], in_=RS[:], func=AF.Sqrt, bias=LN_EPS), R=[RSb], W=[RSb])
                sb.op("dve", lambda e: e.reciprocal(out=RS[:], in_=RS[:]), R=[RSb], W=[RSb])
                for c in range(8):
                    X, Xb = self.XC[c % 2]
                    sb.op("dve", lambda e: e.tensor_tensor(out=X[:], in0=HA[:, c, :], in1=self.mean.ap(T), op=ALU.subtract), R=[HAb, self.mean.b], W=[Xb])
                    sb.op("dve", lambda e: e.tensor_tensor(out=X[:], in0=X[:], in1=RS[:], op=ALU.mult), R=[Xb, RSb], W=[Xb])
                    sb.op("act", lambda e: e.activation(out=HA[:, c, :], in_=X[:], func=AF.Identity,
                                                        scale=der(self.l, self.s, self.kb, c), bias=der(self.l, self.s, self.kb + 1, c)),
                          R=[Xb, DERb], W=[HAb])
                    if self.has_hm:
                        sb.op("act", lambda e: e.activation(out=HM[:, c, :], in_=X[:], func=AF.Identity,
                                                            scale=der(self.l, self.s, self.kb + 2, c), bias=der(self.l, self.s, self.kb + 3, c)),
                              R=[Xb, DERb], W=[HMb[c]])

        YB = [PA0, PA1, PB0, PB1]

        def run_ffn(l, ti, HA, HAb, HM, HMb):
            T, s, rl = ti["T"], ti["s"], ti["rl"]
            with Stage(sb, "ffn") as st:
                U, Ub = st.tile([128, 22, T], BF16)
                AT = [st.tile([128, T], F32) for _ in range(2)]
                GT = [st.tile([128, T], F32) for _ in range(2)]
                ln = LN(st, T, HA, HAb, HM, HMb, l, s, 1, dve_sum=True)
                ybi = 0

                def conv(bank, X, Xb, j):
                    w0, w1, w2, cb = vcol(("cw", l, 0), j), vcol(("cw", l, 1), j), vcol(("cw", l, 2), j), vcol(("cb", l), j)
                    sb.op("act", lambda e: e.activation(out=X[:], in_=bank.ap(T), func=AF.Identity, scale=w1, bias=cb), R=[bank.b, VECb], W=[Xb])
                    pv = bank.ap(T).rearrange("p (r l) -> p r l", l=rl)
                    xv = X[:].rearrange("p (r l) -> p r l", l=rl)
                    sb.op("dve", lambda e: e.scalar_tensor_tensor(out=xv[:, :, 1:rl], in0=pv[:, :, 0:rl - 1], scalar=w0, in1=xv[:, :, 1:rl],
                                                                   op0=ALU.mult, op1=ALU.add), R=[bank.b, VECb, Xb], W=[Xb])
                    sb.op("dve", lambda e: e.scalar_tensor_tensor(out=xv[:, :, 0:rl - 1], in0=pv[:, :, 1:rl], scalar=w2, in1=xv[:, :, 0:rl - 1],
                                                                   op0=ALU.mult, op1=ALU.add), R=[bank.b, VECb, Xb], W=[Xb])

                for g in range(11):
                    slot = ws.next()
                    wa = wview(slot, 0, 8, 256)
                    wg = wview(slot, 2048, 8, 256)
                    for jj in range(2):
                        j = 2 * g + jj
                        ba, bg = YB[ybi % 4], YB[(ybi + 1) % 4]
                        ybi += 2
                        for kc in range(8):
                            sb.op("pe", lambda e: e.matmul(ba.ap(T), lhsT=wa[:, kc, jj * 128:(jj + 1) * 128], rhs=HM[:, kc, :], start=(kc == 0), stop=(kc == 7)),
                                  R=[slot[1], HMb[kc]], W=[ba.b], inc=(kc == 7))
                        for kc in range(8):
                            sb.op("pe", lambda e: e.matmul(bg.ap(T), lhsT=wg[:, kc, jj * 128:(jj + 1) * 128], rhs=HM[:, kc, :], start=(kc == 0), stop=(kc == 7)),
                                  R=[slot[1], HMb[kc]], W=[bg.b], inc=(kc == 7))
                        A, Ab = AT[j % 2]
                        G, Gb = GT[j % 2]
                        conv(ba, A, Ab, j)
                        conv(bg, G, Gb, 22 + j)
                        sb.op("act", lambda e: e.activation(out=G[:], in_=G[:], func=AF.Silu), R=[Gb], W=[Gb])
                        sb.op("dve", lambda e: e.tensor_tensor(out=U[:, j, :], in0=G[:], in1=A[:], op=ALU.mult), R=[Gb, Ab], W=[Ub])
                for g in range(4):
                    slot = ws.next()
                    wd = wview(slot, 0, 22, 256)
                    for oo in range(2):
                        oc = 2 * g + oo
                        yb = YB[ybi % 4]
                        ybi += 1
                        for j in range(22):
                            sb.op("pe", lambda e: e.matmul(yb.ap(T), lhsT=wd[:, j, oo * 128:(oo + 1) * 128], rhs=U[:, j, :], start=(j == 0), stop=(j == 21)),
                                  R=[slot[1], Ub], W=[yb.b], inc=(j == 21))
                        ln.accum(oc, yb)
                ln.finish()

        def run_sgu(l, ti, HA, HAb, HM, HMb):
            idx = l // 2
            T, s = ti["T"], ti["s"]
            NG = T // 128
            with Stage(sb, "sgu") as st:
                VT, VTb = st.tile([128, NG, 3072], BF16)
                UF, UFb = st.tile([128, 24, T], BF16)
                WSF, WSFb = st.tile([128, 8, 128], F32)
                WSB, WSBb = st.tile([128, 8, 128], BF16)
                BSB, BSBb = st.tile([128, 8, 128], F32)
                BT, BTb = st.tile([128, 24, 128], F32)
                GEL = [st.tile([128, 512], F32) for _ in range(2)]
                STATS, STATSb = st.tile([128, NG, 6, 6], F32)
                MV, MVb = st.tile([128, NG, 2], F32)
                TMP = [st.tile([128, T], F32) for _ in range(2)]
                ln = LN(st, T, HA, HAb, HM, HMb, l, s, 0, dve_sum=True)
                sb.dma("sp", WSF[:], sgu_wsT[idx], WD, WSFb, WSFb)
                sb.op("dve", lambda e: e.tensor_copy(out=WSB[:], in_=WSF[:]), R=[WSFb], W=[WSBb])
                sb.dma("sp", BSB[:].rearrange("p g q -> p (g q)"), sgu_bs[idx].rearrange("g q -> (g q)").partition_broadcast(128), WD, BSBb, BSBb)
                for g in range(8):
                    bank = PD
                    sb.op("pe", lambda e: e.matmul(bank.ap(128), lhsT=ONES, rhs=WSF[:, g, :], start=True, stop=True), R=[CSTb, WSFb], W=[bank.b])
                    for cc in range(3):
                        c = 3 * g + cc
                        sb.op("dve", lambda e: e.scalar_tensor_tensor(out=BT[:, c, :], in0=bank.ap(128), scalar=vcol(("sb", idx), c), in1=BSB[:, g, :],
                                                                       op0=ALU.mult, op1=ALU.add), R=[bank.b, VECb, BSBb], W=[BTb])
                vb = [PA0, PA1, PB0, PB1]
                vi = 0
                for g in range(6):
                    slot = ws.next()
                    w = wview(slot, 0, 8, 512)
                    for tg in range(NG):
                        bank = vb[vi % 4]
                        vi += 1
                        for kc in range(8):
                            sb.op("pe", lambda e: e.matmul(bank.ap(512), lhsT=HM[:, kc, tg * 128:(tg + 1) * 128], rhs=w[:, kc, :], start=(kc == 0), stop=(kc == 7)),
                                  R=[slot[1], HMb[kc]], W=[bank.b], inc=(kc == 7))
                        GE, GEb = GEL[vi % 2]
                        sb.op("act", lambda e: e.activation(out=GE[:], in_=bank.ap(512), func=AF.Gelu), R=[bank.b], W=[GEb])
                        sb.op("dve", lambda e: e.bn_stats(out=STATS[:, tg, g, :], in_=GE[:]), R=[GEb], W=[STATSb])
                        sb.op("dve", lambda e: e.tensor_copy(out=VT[:, tg, g * 512:(g + 1) * 512], in_=GE[:]), R=[GEb], W=[VTb])
                for tg in range(NG):
                    sb.op("dve", lambda e: e.bn_aggr(out=MV[:, tg, :], in_=STATS[:, tg, :, :].rearrange("p a b -> p (a b)")), R=[STATSb], W=[MVb])
                sb.op("act", lambda e: e.activation(out=MV[:, :, 1], in_=MV[:, :, 1], func=AF.Sqrt, bias=LN_EPS), R=[MVb], W=[MVb])
                sb.op("dve", lambda e: e.reciprocal(out=MV[:, :, 1], in_=MV[:, :, 1]), R=[MVb], W=[MVb])
                for tg in range(NG):
                    sb.op("dve", lambda e: e.tensor_scalar(out=VT[:, tg, :], in0=VT[:, tg, :], scalar1=MV[:, tg, 0:1], scalar2=MV[:, tg, 1:2],
                                                           op0=ALU.subtract, op1=ALU.mult), R=[VTb, MVb], W=[VTb])
                ub = [PA0, PA1]
                mb = [PB0, PB1]
                for g in range(6):
                    slot = ws.next()
                    w = wview(slot, 0, 8, 512)
                    for f in range(4):
                        c = 4 * g + f
                        bank = ub[c % 2]
                        for kc in range(8):
                            sb.op("pe", lambda e: e.matmul(bank.ap(T), lhsT=w[:, kc, f * 128:(f + 1) * 128], rhs=HM[:, kc, :], start=(kc == 0), stop=(kc == 7)),
                                  R=[slot[1], HMb[kc]], W=[bank.b], inc=(kc == 7))
                        sb.op("act", lambda e: e.activation(out=UF[:, c, :], in_=bank.ap(T), func=AF.Gelu), R=[bank.b], W=[UFb])
                        mbank = mb[c % 2]
                        grp = c // 3
                        for tg in range(NG):
                            sb.op("pe", lambda e: e.matmul(mbank.ap(128, tg * 128), lhsT=VT[:, tg, c * 128:(c + 1) * 128], rhs=WSB[:, grp, :], start=True, stop=True),
                                  R=[VTb, WSBb], W=[mbank.b], inc=(tg == NG - 1))
                        TM, TMb = TMP[c % 2]
                        sb.op("dve", lambda e: e.scalar_tensor_tensor(
                            out=TM[:].rearrange("p (g q) -> p g q", q=128), in0=mbank.ap(T).rearrange("p (g q) -> p g q", q=128),
                            scalar=vcol(("sg", idx), c), in1=BT[:, c:c + 1, :].to_broadcast([128, NG, 128]),
                            op0=ALU.mult, op1=ALU.add), R=[mbank.b, VECb, BTb], W=[TMb])
                        sb.op("dve", lambda e: e.tensor_tensor(out=UF[:, c, :], in0=UF[:, c, :], in1=TM[:], op=ALU.mult), R=[UFb, TMb], W=[UFb])
                ybi = 0
                for g in range(4):
                    slot = ws.next()
                    wo = wview(slot, 0, 24, 256)
                    for oo in range(2):
                        oc = 2 * g + oo
                        yb = YB[ybi % 4]
                        ybi += 1
                        for j in range(24):
                            sb.op("pe", lambda e: e.matmul(yb.ap(T), lhsT=wo[:, j, oo * 128:(oo + 1) * 128], rhs=UF[:, j, :], start=(j == 0), stop=(j == 23)),
                                  R=[slot[1], UFb], W=[yb.b], inc=(j == 23))
                        ln.accum(oc, yb)
                ln.finish()

        def run_hg1(l, i, ti, HM, HMb):
            idx = l // 2
            T = ti["T"]
            NG = T // 128
            with Stage(sb, "hg1") as so:
                Q, Qb = so.tile([128, 8, T], F32)
                VT, VTb = so.tile([128, NG, 1024], BF16)
                OL, OLb = so.tile([128, 8, T], F32)
                with Stage(sb, "hg1a") as st:
                    GTt, GTb = st.tile([128, 8, T], BF16)
                    banks = [PA0, PA1, PB0, PB1]
                    bi = 0
                    for g in range(6):
                        slot = ws.next()
                        w = wview(slot, 0, 8, 512)
                        if g < 4:
                            for f in range(4):
                                h = (g % 2) * 4 + f
                                bank = banks[bi % 4]
                                bi += 1
                                for kc in range(8):
                                    sb.op("pe", lambda e: e.matmul(bank.ap(T), lhsT=w[:, kc, f * 128:(f + 1) * 128], rhs=HM[:, kc, :], start=(kc == 0), stop=(kc == 7)),
                                          R=[slot[1], HMb[kc]], W=[bank.b], inc=(kc == 7))
                                if g < 2:
                                    sb.op("act", lambda e: e.activation(out=Q[:, h, :], in_=bank.ap(T), func=AF.Silu), R=[bank.b], W=[Qb])
                                else:
                                    sb.op("act", lambda e: e.activation(out=GTt[:, h, :], in_=bank.ap(T), func=AF.Silu), R=[bank.b], W=[GTb])
                        else:
                            for tg in range(NG):
                                bank = banks[bi % 4]
                                bi += 1
                                for kc in range(8):
                                    sb.op("pe", lambda e: e.matmul(bank.ap(512), lhsT=HM[:, kc, tg * 128:(tg + 1) * 128], rhs=w[:, kc, :], start=(kc == 0), stop=(kc == 7)),
                                          R=[slot[1], HMb[kc]], W=[bank.b], inc=(kc == 7))
                                sb.op("act", lambda e: e.activation(out=VT[:, tg, (g - 4) * 512:(g - 3) * 512], in_=bank.ap(512), func=AF.Identity), R=[bank.b], W=[VTb])
                    sb.dma("sp", GATE[i][0][:, :, 0:T], GTt[:], GTb, GATE[i][1], GTb)
                OLB = [Buf("olA"), Buf("olB")]
                so.bufs.extend(OLB)
                for d in range(2):
                    with Stage(sb, "hg1b") as st:
                        def T2(shape, dt):
                            return [st.tile(shape, dt) for _ in range(2)]
                        QTI = T2([128, 4, T], BF16)
                        LOGF = T2([128, 512], F32)
                        KK = T2([128, 512], F32)
                        CAR = T2([128, 512], F32)
                        VM = T2([128, 4, 512], BF16)
                        ENG = T2([128, 512], F32)
                        KT = T2([128, 512], BF16)
                        KTT = T2([128, 4, 128], BF16)
                        KH = T2([128, 512], BF16)
                        QTS_ = T2([128, 4, 128], BF16)
                        EG = T2([128, 4, 128], F32)
                        QTL = T2([128, 4, 128], BF16)
                        EGT = T2([128, 4, 128], F32)
                        SCM = T2([128, 4, 128], BF16)
                        S = T2([128, 4, 128], F32)
                        SBF = T2([128, 4, 128], BF16)
                        TMP = T2([128, 4, 128], F32)
                        FF = T2([128, 512], F32)
                        DT = T2([128, 4], F32)
                        MSK, MSKb = st.tile([128, 4, 128], F32)
                        OM = T2([128, 512], F32) if idx > 0 else None
                        sb.op("dve", lambda e: e.tensor_copy(out=MSK[:], in_=UBD[d].unsqueeze(1).to_broadcast([128, 4, 128])), R=[CSTb], W=[MSKb])
                        wslots = [ws.next(), ws.next(keep=1)]
                        order = list(range(NG)) if d == 0 else list(range(NG - 1, -1, -1))
                        jorder = [0, 1, 2, 3] if d == 0 else [3, 2, 1, 0]
                        PTB = [PTb, PTb]

                        def chain(hh):
                            hs = hh * 4
                            cs = slice(hh * 512, (hh + 1) * 512)
                            PAh, PBh, PCh = (PA0, PA1)[hh], (PB0, PB1)[hh], (PC0, PC1)[hh]
                            PTh, PThb = PTt[:, hh * 512:(hh + 1) * 512], PTB[hh]
                            slot = wslots[hh]
                            w = wview(slot, 0, 8, 512)
                            qti, qtib = QTI[hh]
                            logf, logfb = LOGF[hh]
                            kk, kkb = KK[hh]
                            car, carb = CAR[hh]
                            vm, vmb = VM[hh]
                            eng, engb = ENG[hh]
                            kt, ktb = KT[hh]
                            ktt, kttb = KTT[hh]
                            kh, khb = KH[hh]
                            qts, qtsb = QTS_[hh]
                            eg, egb = EG[hh]
                            qtl, qtlb = QTL[hh]
                            egt, egtb = EGT[hh]
                            scm, scmb = SCM[hh]
                            s_, s_b = S[hh]
                            sbf, sbfb = SBF[hh]
                            tmp, tmpb = TMP[hh]
                            ff, ffb = FF[hh]
                            dt_, dtb = DT[hh]
                            olb = OLB[hh]
                            f2 = lambda a: a.rearrange("p h s -> p (h s)")
                            v3 = lambda a: a.rearrange("p (h s) -> p h s", h=4)
                            sb.op("dve", lambda e: e.memset(scm[:], 0.0), W=[scmb])
                            if idx > 0:
                                om, omb = OM[hh]
                                sb.dma("sp", om[:], hg_lb[d, 0, cs].partition_broadcast(128), WD, omb, omb)
                                sb.dma("sp", eng[:], hg_lb[d, 1, cs].partition_broadcast(128), WD, engb, engb)
                                yield
                                sb.op("dve", lambda e: e.tensor_tensor(out=om[:], in0=om[:], in1=eng[:], op=ALU.subtract), R=[omb, engb], W=[omb])
                                yield
                                sb.op("act", lambda e: e.activation(out=om[:], in_=om[:], func=AF.Sigmoid), R=[omb], W=[omb])
                                yield
                            first_step = True
                            for gi_, tg in enumerate(order):
                                tsl = slice(tg * 128, (tg + 1) * 128)
                                for kc in range(8):
                                    sb.op("pe", lambda e: e.matmul(PAh.ap(512), lhsT=HM[:, kc, tsl], rhs=w[:, kc, :], start=(kc == 0), stop=(kc == 7)),
                                          R=[slot[1], HMb[kc]], W=[PAh.b], inc=(kc == 7))
                                yield
                                if idx == 0:
                                    sb.op("act", lambda e: e.activation(out=ff[:], in_=PAh.ap(512), func=AF.Sigmoid), R=[PAh.b], W=[ffb])
                                    sb.op("act", lambda e: e.activation(out=kk[:], in_=PAh.ap(512), func=AF.Sigmoid, scale=-1.0), R=[PAh.b], W=[kkb])
                                    yield
                                else:
                                    sb.op("act", lambda e: e.activation(out=ff[:], in_=PAh.ap(512), func=AF.Sigmoid, scale=-1.0), R=[PAh.b], W=[ffb])
                                    yield
                                    sb.op("dve", lambda e: e.tensor_tensor(out=kk[:], in0=ff[:], in1=om[:], op=ALU.mult), R=[ffb, omb], W=[kkb])
                                    yield
                                    sb.op("dve", lambda e: e.tensor_scalar(out=ff[:], in0=kk[:], scalar1=-1.0, scalar2=1.0, op0=ALU.mult, op1=ALU.add), R=[kkb], W=[ffb])
                                    yield
                                sb.op("act", lambda e: e.activation(out=logf[:], in_=ff[:], func=AF.Ln), R=[ffb], W=[logfb])
                                yield
                                for j in range(4):
                                    sb.op("act", lambda e: e.activation(out=vm[:, j, :], in_=VT[:, tg, cs], func=AF.Identity, scale=RMASK[:, j:j + 1]), R=[VTb, CSTb], W=[vmb])
                                yield
                                sb.op("pe", lambda e: e.matmul(PAh.ap(512), lhsT=AMID[d], rhs=logf[:], start=True, stop=True), R=[CSTb, logfb], W=[PAh.b])
                                yield
                                sb.op("act", lambda e: e.activation(out=eng[:], in_=PAh.ap(512), func=AF.Exp, scale=-1.0), R=[PAh.b], W=[engb])
                                yield
                                sb.op("dve", lambda e: e.tensor_tensor(out=kt[:], in0=kk[:], in1=eng[:], op=ALU.mult), R=[kkb, engb], W=[ktb])
                                yield
                                sb.op("pe", lambda e: e.matmul(PAh.ap(512), lhsT=SUPM[d], rhs=logf[:], start=True, stop=True), R=[CSTb, logfb], W=[PAh.b])
                                yield
                                sb.op("act", lambda e: e.activation(out=eng[:], in_=PAh.ap(512), func=AF.Exp), R=[PAh.b], W=[engb])
                                yield
                                sb.op("dve", lambda e: e.tensor_tensor(out=kh[:], in0=kk[:], in1=eng[:], op=ALU.mult), R=[kkb, engb], W=[khb])
                                yield
                                for h in range(4):
                                    sb.op("pe", lambda e: e.transpose(PTh[:, h * 128:(h + 1) * 128], kt[:, h * 128:(h + 1) * 128], IDB[:]),
                                          R=[ktb, IDBb], W=[PThb], inc=(h == 3))
                                yield
                                sb.op("dve", lambda e: e.tensor_copy(out=f2(ktt[:]), in_=PTh), R=[PThb], W=[kttb])
                                yield
                                for h in range(4):
                                    sb.op("pe", lambda e: e.matmul(PBh.ap(128, h * 128), lhsT=logf[:, h * 128:(h + 1) * 128], rhs=AMID[d], start=True, stop=True),
                                          R=[CSTb, logfb], W=[PBh.b], inc=(h == 3))
                                yield
                                sb.op("act", lambda e: e.activation(out=f2(egt[:]), in_=PBh.ap(512), func=AF.Exp), R=[PBh.b], W=[egtb])
                                yield
                                sb.op("dve", lambda e: e.tensor_tensor(out=qtl[:], in0=Q[:, hs:hs + 4, tsl], in1=egt[:], op=ALU.mult), R=[Qb, egtb], W=[qtlb])
                                yield
                                for h in range(4):
                                    sb.op("pe", lambda e: e.matmul(PBh.ap(128, h * 128), lhsT=logf[:, h * 128:(h + 1) * 128], rhs=UBD[d], start=True, stop=True),
                                          R=[CSTb, logfb], W=[PBh.b], inc=(h == 3))
                                yield
                                sb.op("act", lambda e: e.activation(out=f2(eg[:]), in_=PBh.ap(512), func=AF.Exp), R=[PBh.b], W=[egb])
                                yield
                                sb.op("dve", lambda e: e.tensor_tensor(out=qts[:], in0=Q[:, hs:hs + 4, tsl], in1=eg[:], op=ALU.mult), R=[Qb, egb], W=[qtsb])
                                yield
                                for h in range(4):
                                    if gi_ > 0:
                                        sb.op("pe", lambda e: e.matmul(PCh.ap(128, h * 128), lhsT=car[:, h * 128:(h + 1) * 128], rhs=ONES, start=True, stop=False),
                                              R=[CSTb, carb], W=[PCh.b], inc=False)
                                    sb.op("pe", lambda e: e.matmul(PCh.ap(128, h * 128), lhsT=logf[:, h * 128:(h + 1) * 128], rhs=UTR[d], start=(gi_ == 0), stop=True),
                                          R=[CSTb, logfb], W=[PCh.b], inc=(h == 3))
                                yield
                                sb.op("act", lambda e: e.activation(out=f2(egt[:]), in_=PCh.ap(512), func=AF.Exp), R=[PCh.b], W=[egtb])
                                yield
                                sb.op("dve", lambda e: e.tensor_tensor(out=qti[:, :, tsl], in0=Q[:, hs:hs + 4, tsl], in1=egt[:], op=ALU.mult), R=[Qb, egtb], W=[qtib])
                                if gi_ == 0:
                                    sb.op("dve", lambda e: e.tensor_copy(out=car[:], in_=logf[:]), R=[logfb], W=[carb])
                                elif gi_ < NG - 1:
                                    sb.op("dve", lambda e: e.tensor_tensor(out=car[:], in0=car[:], in1=logf[:], op=ALU.add), R=[logfb, carb], W=[carb])
                                if gi_ == NG - 1:
                                    ecol = 127 if d == 0 else 0
                                    sb.op("dve", lambda e: e.tensor_copy(out=dt_[:], in_=egt[:, :, ecol]), R=[egtb], W=[dtb])
                                yield
                                for h in range(4):
                                    sb.op("pe", lambda e: e.matmul(PAh.ap(128, h * 128), lhsT=ktt[:, h, :], rhs=qtl[:, h, :], start=True, stop=True),
                                          R=[kttb, qtlb], W=[PAh.b], inc=(h == 3))
                                yield
                                sb.op("dve", lambda e: e.copy_predicated(out=scm[:], mask=MSK[:].bitcast(mybir.dt.uint32), data=v3(PAh.ap(512))),
                                      R=[PAh.b, MSKb, scmb], W=[scmb])
                                yield
                                for h in range(4):
                                    sb.op("pe", lambda e: e.matmul(PBh.ap(128, h * 128), lhsT=VT[:, tg, (hs + h) * 128:(hs + h + 1) * 128], rhs=scm[:, h, :], start=(h == 0), stop=False),
                                          R=[VTb, scmb], W=[PBh.b], inc=(h == 3))
                                yield
                                for j in jorder:
                                    jsl = slice(j * 32, (j + 1) * 32)
                                    if not first_step:
                                        for h in range(4):
                                            sb.op("pe", lambda e: e.matmul(PBh.ap(32, h * 128 + j * 32), lhsT=sbf[:, h, :], rhs=qts[:, h, jsl], start=False, stop=True),
                                                  R=[sbfb, qtsb], W=[PBh.b], inc=(h == 3))
                                    for h in range(4):
                                        sb.op("pe", lambda e: e.matmul(PCh.ap(128, h * 128), lhsT=kh[:, h * 128:(h + 1) * 128], rhs=vm[:, j, h * 128:(h + 1) * 128], start=True, stop=True),
                                              R=[khb, vmb], W=[PCh.b], inc=(h == 3))
                                    yield
                                    ce = j * 32 + 31 if d == 0 else j * 32
                                    ev = eg[:, :, ce:ce + 1].to_broadcast([128, 4, 128])
                                    dsv = v3(PCh.ap(512))
                                    if first_step:
                                        sb.op("dve", lambda e: e.tensor_copy(out=s_[:], in_=dsv), R=[PCh.b], W=[s_b])
                                        yield
                                    else:
                                        sb.op("dve", lambda e: e.tensor_tensor(out=tmp[:], in0=s_[:], in1=ev, op=ALU.mult), R=[s_b, egb], W=[tmpb])
                                        yield
                                        sb.op("dve", lambda e: e.tensor_tensor(out=s_[:], in0=dsv, in1=tmp[:], op=ALU.add), R=[PCh.b, tmpb], W=[s_b])
                                        yield
                                    sb.op("act", lambda e: e.activation(out=f2(sbf[:]), in_=f2(s_[:]), func=AF.Identity), R=[s_b], W=[sbfb])
                                    yield
                                    first_step = False
                                ov = v3(PBh.ap(512))
                                if d == 0:
                                    sb.op("dve", lambda e: e.tensor_copy(out=OL[:, hs:hs + 4, tsl], in_=ov), R=[PBh.b], W=[olb])
                                else:
                                    sb.op("dve", lambda e: e.tensor_tensor(out=OL[:, hs:hs + 4, tsl], in0=ov, in1=OL[:, hs:hs + 4, tsl], op=ALU.add), R=[PBh.b, olb], W=[olb])
                                yield
                            sb.dma("sp", LST[i][d][0][:, hs:hs + 4, :], s_[:], s_b, LST[i][d][1], s_b)
                            sb.dma("sp", DST[i][d][0][:, hs:hs + 4], dt_[:], dtb, DST[i][d][1], dtb)
                            sb.dma("sp", QTS[i][d][0][:, hs:hs + 4, 0:T], qti[:], qtib, QTS[i][d][1], qtib)

                        gens = [chain(0), chain(1)]
                        alive = [True, True]
                        while any(alive):
                            for gi2 in range(2):
                                if alive[gi2]:
                                    try:
                                        next(gens[gi2])
                                    except StopIteration:
                                        alive[gi2] = False
                for hh in range(2):
                    sb.dma("sp", OLOC[i][0][:, hh * 4:hh * 4 + 4, 0:T], OL[:, hh * 4:hh * 4 + 4, :], OLB[hh], OLOC[i][1], OLB[hh])

        def run_boundary(l):
            with Stage(sb, "bnd") as st:
                SF, SFb = st.tile([128, 8, 128], F32)
                LT, LTb = st.tile([128, 8, 128], F32)
                DTt, DTtb = st.tile([128, 8], F32)
                S0 = [st.tile([128, 8, 128], F32) for _ in range(2)]
                EX = [st.tile([128, 8, 128], F32) for _ in range(2)]
                SB16, SB16b = st.tile([128, 8, 128], BF16)

                def recur(d, init, initb, store):
                    order = list(range(NT)) if d == 0 else list(range(NT - 1, -1, -1))
                    sb.op("dve", lambda e: e.tensor_copy(out=SF[:], in_=init[:]), R=[initb], W=[SFb])
                    for i in order:
                        if store:
                            sb.op("act", lambda e: e.activation(out=SB16[:].rearrange("p h s -> p (h s)"), in_=SF[:].rearrange("p h s -> p (h s)"), func=AF.Identity), R=[SFb], W=[SB16b])
                            sb.dma("sp", SIN[i][d][0], SB16[:], SB16b, SIN[i][d][1], SB16b)
                        sb.dma("sp", LT[:], LST[i][d][0], LST[i][d][1], LTb, LTb)
                        sb.dma("sp", DTt[:], DST[i][d][0], DST[i][d][1], DTtb, DTtb)
                        sb.op("dve", lambda e: e.tensor_tensor(out=SF[:], in0=SF[:], in1=DTt[:].unsqueeze(2).to_broadcast([128, 8, 128]), op=ALU.mult), R=[SFb, DTtb], W=[SFb])
                        sb.op("dve", lambda e: e.tensor_tensor(out=SF[:], in0=SF[:], in1=LT[:], op=ALU.add), R=[SFb, LTb], W=[SFb])

                for d in range(2):
                    sb.dma("sp", S0[d][0][:], LST[CTX][d][0], LST[CTX][d][1], S0[d][1], S0[d][1])
                for d in range(2):
                    recur(d, S0[d][0], S0[d][1], False)
                    sb.dma("sp", CCI[0][d * 128:(d + 1) * 128, :].rearrange("p (h s) -> p h s", h=8), SF[:], SFb, CCI[1], SFb)
                sb.op("pool", lambda e: e.collective_compute("AllGather", ALU.bypass, replica_groups=[[0, 1], [2, 3], [4, 5], [6, 7]],
                                                             ins=[CCI[0].opt()], outs=[CCO[0].opt()]), R=[CCI[1]], W=[CCO[1]])
                sb.dma("sp", EX[0][0][:], CCO[0][0:128, :].rearrange("p (h s) -> p h s", h=8), CCO[1], EX[0][1], EX[0][1])
                sb.dma("sp", EX[1][0][:], CCO[0][384:512, :].rearrange("p (h s) -> p h s", h=8), CCO[1], EX[1][1], EX[1][1])
                for d in range(2):
                    own, ownb = S0[d]
                    oth, othb = EX[d]
                    a, b_ = (0, 1) if d == 0 else (1, 0)
                    sb.op("dve", lambda e: e.tensor_scalar_mul(out=own[:], in0=own[:], scalar1=SEL[:, a:a + 1]), R=[ownb, SELb], W=[ownb])
                    sb.op("dve", lambda e: e.scalar_tensor_tensor(out=own[:], in0=oth[:], scalar=SEL[:, b_:b_ + 1], in1=own[:], op0=ALU.mult, op1=ALU.add),
                          R=[othb, SELb, ownb], W=[ownb])
                    recur(d, own, ownb, True)

        def run_hg2(l, i, ti, HA, HAb, HM, HMb):
            idx = l // 2
            T, s = ti["T"], ti["s"]
            corr = (i != CTX)
            with Stage(sb, "hg2") as st:
                O, Ob = st.tile([128, 8, T], F32)
                GTt, GTb = st.tile([128, 8, T], BF16)
                OSQ, OSQb = st.tile([128, 8, T], F32)
                RS, RSb = st.tile([128, 8, T], F32)
                R, Rb = st.tile([128, 8, T], BF16)
                ln = LN(st, T, HA, HAb, HM, HMb, l, s, 0)
                sb.dma("sp", O[:], OLOC[i][0][:, :, 0:T], OLOC[i][1], Ob, Ob)
                sb.dma("sp", GTt[:], GATE[i][0][:, :, 0:T], GATE[i][1], GTb, GTb)
                if corr:
                    QD = [st.tile([128, 8, T], BF16) for _ in range(2)]
                    SI = [st.tile([128, 8, 128], BF16) for _ in range(2)]
                    for d in range(2):
                        sb.dma("sp", QD[d][0][:], QTS[i][d][0][:, :, 0:T], QTS[i][d][1], QD[d][1], QD[d][1])
                        sb.dma("sp", SI[d][0][:], SIN[i][d][0], SIN[i][d][1], SI[d][1], SI[d][1])
                    for h in range(8):
                        bank = YB[h % 4]
                        for d in range(2):
                            sb.op("pe", lambda e: e.matmul(bank.ap(T), lhsT=SI[d][0][:, h, :], rhs=QD[d][0][:, h, :], start=(d == 0), stop=(d == 1)),
                                  R=[SI[d][1], QD[d][1]], W=[bank.b], inc=(d == 1))
                        sb.op("dve", lambda e: e.tensor_tensor(out=O[:, h, :], in0=bank.ap(T), in1=O[:, h, :], op=ALU.add), R=[bank.b, Ob], W=[Ob])
                for h in range(8):
                    sb.op("act", lambda e: e.activation(out=OSQ[:, h, :], in_=O[:, h, :], func=AF.Square), R=[Ob], W=[OSQb])
                    bank = YB[h % 4]
                    sb.op("pe", lambda e: e.matmul(bank.ap(T), lhsT=ONESR[:], rhs=OSQ[:, h, :], start=True, stop=True), R=[ONESRb, OSQb], W=[bank.b])
                    sb.op("act", lambda e: e.activation(out=RS[:, h, :], in_=bank.ap(T), func=AF.Sqrt, bias=RMS_EPS), R=[bank.b], W=[RSb])
                    sb.op("dve", lambda e: e.reciprocal(out=RS[:, h, :], in_=RS[:, h, :]), R=[RSb], W=[RSb])
                    sb.op("dve", lambda e: e.tensor_tensor(out=O[:, h, :], in0=O[:, h, :], in1=RS[:, h, :], op=ALU.mult), R=[Ob, RSb], W=[Ob])
                    sb.op("dve", lambda e: e.scalar_tensor_tensor(out=R[:, h, :], in0=O[:, h, :], scalar=vcol(("nw", idx)), in1=GTt[:, h, :], op0=ALU.mult, op1=ALU.mult),
                          R=[Ob, VECb, GTb], W=[Rb])
                ybi = 0
                for g in range(2):
                    slot = ws.next()
                    w = wview(slot, 0, 8, 512)
                    for f in range(4):
                        oc = 4 * g + f
                        yb = YB[ybi % 4]
                        ybi += 1
                        for kc in range(8):
                            sb.op("pe", lambda e: e.matmul(yb.ap(T), lhsT=w[:, kc, f * 128:(f + 1) * 128], rhs=R[:, kc, :], start=(kc == 0), stop=(kc == 7)),
                                  R=[slot[1], Rb], W=[yb.b], inc=(kc == 7))
                        ln.accum(oc, yb)
                ln.finish()

        sched = []

        def ctx_needed_in_loop(l):
            return l + 2 < DEPTH

        l = 0
        for i in [CTX] + list(range(NT)):
            sched.append(("in0", i))
        while l < DEPTH:
            sched.append(("bnd", l))
            tl = ([CTX] if ctx_needed_in_loop(l) else []) + list(range(NT))
            for i in tl:
                sched.append(("tile", l, i))
            l += 2

        for it in sched:
            if it[0] == "in0":
                plan_hg1(0)
            elif it[0] == "tile":
                l = it[1]
                plan_hg2(l // 2)
                plan_ffn(l)
                if l + 1 < DEPTH:
                    plan_sgu((l + 1) // 2)
                    plan_ffn(l + 1)
                if l + 2 < DEPTH:
                    plan_hg1((l + 2) // 2)

        def load_x(ti, X, Xb):
            sb.dma("sp", X[:], ti["src"], WD, Xb, Xb)

        for it in sched:
            if it[0] == "in0":
                i = it[1]
                ti = tinfo(i)
                T, s = ti["T"], ti["s"]
                with Stage(sb, "t0") as tsg:
                    X, Xb = tsg.tile([128, 8, T], F32)
                    HM, HMb0 = tsg.tile([128, 8, T], BF16)
                    HMb = [HMb0] + [Buf("hmc") for _ in range(7)]
                    tsg.bufs.extend(HMb[1:])
                    load_x(ti, X, Xb)
                    for c in range(8):
                        sb.op("act", lambda e: e.activation(out=HM[:, c, :], in_=X[:, c, :], func=AF.Identity, scale=der(0, s, 0, c), bias=der(0, s, 1, c)),
                              R=[Xb, DERb], W=[HMb[c]])
                    run_hg1(0, i, ti, HM, HMb)
            elif it[0] == "bnd":
                run_boundary(it[1])
            else:
                l, i = it[1], it[2]
                ti = tinfo(i)
                T, s = ti["T"], ti["s"]
                with Stage(sb, "tl") as tsg:
                    HA, HAb = tsg.tile([128, 8, T], F32)
                    HM, HMb0 = tsg.tile([128, 8, T], BF16)
                    HMb = [HMb0] + [Buf("hmc") for _ in range(7)]
                    tsg.bufs.extend(HMb[1:])
                    if l == 0:
                        load_x(ti, HA, HAb)
                        sb.op("act", lambda e: e.activation(out=HA[:].rearrange("p c t -> p (c t)"), in_=HA[:].rearrange("p c t -> p (c t)"), func=AF.Identity, scale=ALPHA),
                              R=[HAb], W=[HAb])
                    else:
                        sb.dma("sp", HA[:], HSCR[i][0], HSCR[i][1], HAb, HAb)
                    run_hg2(l, i, ti, HA, HAb, HM, HMb)
                    dump('h1_%d_%d' % (l, i), HA[:], HAb, [128, 8, T])
                    dump('hm1_%d_%d' % (l, i), HM[:], HMb[7], [128, 8, T], BF16)
                    run_ffn(l, ti, HA, HAb, HM, HMb)
                    dump('h2_%d_%d' % (l, i), HA[:], HAb, [128, 8, T])
                    if l + 1 < DEPTH:
                        run_sgu(l + 1, ti, HA, HAb, HM, HMb)
                        dump('h3_%d_%d' % (l, i), HA[:], HAb, [128, 8, T])
                        run_ffn(l + 1, ti, HA, HAb, HM, HMb)
                        dump('h4_%d_%d' % (l, i), HA[:], HAb, [128, 8, T])
                    if l + 2 < DEPTH:
                        if i != CTX:
                            sb.dma("sp", HSCR[i][0], HA[:], HAb, HSCR[i][1], HAb)
                        run_hg1(l + 2, i, ti, HM, HMb)
                    elif i != CTX:
                        sb.dma("sp", outT[:, :, i * TT:(i + 1) * TT], HA[:], HAb, OUTB, HAb)
        sb.wait_all("sp", [OUTB, DBGB])
        assert ws.taken == len(ws.plan), (ws.taken, len(ws.plan))
    return nc, sb


def _cols(v):
    v = np.asarray(v, np.float32).reshape(-1, 128)
    return np.ascontiguousarray(v.T)


def make_consts():
    s = np.arange(128)[:, None]
    t = np.arange(128)[None, :]
    same = (s // 32) == (t // 32)
    c = np.zeros((128, NCST), np.float32)
    c[:, 0:128] = np.eye(128)
    c[:, 128:256] = same & (s <= t)
    c[:, 256:384] = same & (s >= t)
    c[:, 384:512] = (s <= t)
    c[:, 512:640] = (s >= t)
    c[:, 640:768] = 1.0
    for j in range(4):
        c[32 * j:32 * (j + 1), 768 + j] = 1.0
    for d in range(2):
        pos = (lambda x: x % 32) if d == 0 else (lambda x: 31 - (x % 32))
        pu, pt = pos(s), pos(t)
        A = np.where((pu > 15) & (pu <= pt), 1.0, 0.0) - np.where((pu > pt) & (pu <= 15), 1.0, 0.0)
        c[:, 772 + 128 * d:900 + 128 * d] = A * same
        c[:, 1028 + 128 * d:1156 + 128 * d] = ((pu > pt) & same)
    return c


def make_inputs(inp, NT, DEPTH):
    NH, NS = (DEPTH + 1) // 2, max(DEPTH // 2, 1)
    VOFF, NV = vec_layout(DEPTH)
    f = lambda k: np.asarray(inp[k], np.float32)
    vec = np.zeros((128, NV), np.float32)

    def put(key, v):
        c = _cols(v)
        vec[:, VOFF[key]:VOFF[key] + c.shape[1]] = c
    for l in range(DEPTH):
        put(("adab", l), f("ada_b")[l])
        for w in range(2):
            put(("lng", l, w), f("ln_g")[l, w])
            put(("lnb", l, w), f("ln_b")[l, w])
        for k in range(3):
            put(("cw", l, k), f("ffn_conv_w")[l, k])
        put(("cb", l), f("ffn_conv_b")[l])
    for i in range(NH):
        put(("nw", i), f("hg_norm_w")[i])
    for i in range(min(NS, DEPTH // 2)):
        put(("sg", i), f("sgu_ln_g")[i])
        put(("sb", i), f("sgu_ln_b")[i])
    shared = dict(
        ada_w=np.ascontiguousarray(f("ada_w")[:DEPTH]), hg_w_in=np.ascontiguousarray(f("hg_w_in")[:NH]),
        hg_w_out=np.ascontiguousarray(f("hg_w_out")[:NH]), sgu_w_in=np.ascontiguousarray(f("sgu_w_in")[:NS]),
        sgu_w_out=np.ascontiguousarray(f("sgu_w_out")[:NS]), ffn_w_up=np.ascontiguousarray(f("ffn_w_up")[:DEPTH]),
        ffn_w_down=np.ascontiguousarray(f("ffn_w_down")[:DEPTH]), vecs=vec, hg_lb=f("hg_lb"),
        sgu_bs=np.ascontiguousarray(f("sgu_b_s")[:NS]),
        sgu_wsT=np.ascontiguousarray(f("sgu_w_s")[:NS].transpose(0, 3, 1, 2)),
        consts=make_consts())
    x, c, ctx, c_ctx = f("x"), f("c"), f("ctx"), f("c_ctx")
    NTT = NT * TT
    maps = []
    for core in range(8):
        b, half = core // 2, core % 2
        xs = x[b, half * NTT:(half + 1) * NTT, :]
        xTc = np.ascontiguousarray(xs.T.reshape(8, 128, NTT).transpose(1, 0, 2))
        cT = np.ascontiguousarray(ctx[b].T.reshape(8, 128, CT).transpose(1, 0, 2))
        cv = np.stack([_cols(c[b]), _cols(c_ctx)], axis=2)
        se = np.zeros((128, 2), np.float32)
        se[:, half] = 1.0
        m = dict(shared)
        m.update(xT=xTc, ctxT=cT, cvec=np.ascontiguousarray(cv), sel=se)
        maps.append(m)
    return maps


_CACHE = {}


def run(inp, NT, DEPTH):
    key = (NT, DEPTH)
    if key not in _CACHE:
        _CACHE[key] = build(NT, DEPTH)[0]
    nc = _CACHE[key]
    maps = make_inputs(inp, NT, DEPTH)
    res = run_bass_kernel_spmd(nc, maps, core_ids=list(range(8)))
    global LAST
    LAST = res.results
    NTT = NT * TT
    B = 4
    out = np.zeros((B, 2 * NTT, 1024), np.float32)
    for core in range(8):
        b, half = core // 2, core % 2
        o = res.results[core]["outT"]
        out[b, half * NTT:(half + 1) * NTT, :] = o.transpose(2, 1, 0).reshape(NTT, 1024)
    return out


def kernel(**inputs):
    x = np.asarray(inputs["x"])
    NT = x.shape[1] // (2 * TT)
    return run(inputs, NT, 4)
```
